# Optimizing a Trainium2 kernel written in Bass

```python
import math
import jax, jax.numpy as jnp
from jax import lax
import numpy as np


D_MODEL = 1024
BATCH = 32
SEQ = 256
DEPTH = 2
DEC_BATCH = 8
DEC_SEQ = 2048
PAST_LEN = 512

GRID_W = 64
HEAD_DIM = 64
A_HEADS = 8
A_KV_HEADS = 2
B_GROUPS = 4
B_GROUP_DIM = 128
B_WIDTH = B_GROUPS * B_GROUP_DIM
CHUNK = 128
C_HEADS = 16
C_KV_HEADS = 4
WINDOW = 128
Q_BLOCK = 128
FFN_DIM = 2816
CONV_W = 3
ROPE_THETA = 10000.0
EPS = 1e-6
NEG = -1e30
N_EVEN = (DEPTH + 1) // 2
N_ODD = DEPTH // 2
A_Q_W = A_HEADS * HEAD_DIM
A_KV_W = A_KV_HEADS * HEAD_DIM
EVEN_IN = A_Q_W + 2 * A_KV_W + 2 * B_WIDTH
EVEN_SPLITS = (A_Q_W, A_Q_W + A_KV_W, A_Q_W + 2 * A_KV_W, A_Q_W + 2 * A_KV_W + B_WIDTH)
EVEN_MIX = A_Q_W + B_WIDTH
C_Q_W = C_HEADS * HEAD_DIM
C_KV_W = C_KV_HEADS * HEAD_DIM
ODD_IN = C_Q_W + 2 * C_KV_W
ODD_SPLITS = (C_Q_W, C_Q_W + C_KV_W)
ODD_MIX = C_Q_W

kernel_name = "hybrid_flow_prefix_trunk_step"


def rms_norm(x, gain=None):
    xf = x.astype(jnp.float32)
    y = xf * lax.rsqrt(jnp.mean(xf * xf, axis=-1, keepdims=True) + EPS)
    if gain is not None:
        y = y * gain.astype(jnp.float32)
    return y.astype(x.dtype)


def modulation(cond, w_mod, b_mod):
    m = jax.nn.silu(cond) @ w_mod + b_mod
    return jnp.split(m[:, None, :], 6, axis=-1)


def adaln(x, shift, scale):
    return rms_norm(x) * (1 + scale) + shift


def axial_rope(n_tokens):
    n_rows = n_tokens // GRID_W
    rows = jnp.repeat(jnp.arange(n_rows), GRID_W).astype(jnp.float32)
    cols = jnp.tile(jnp.arange(GRID_W), n_rows).astype(jnp.float32)
    n_freq = HEAD_DIM // 4
    inv = ROPE_THETA ** (-jnp.arange(n_freq, dtype=jnp.float32) / n_freq)
    ang = jnp.concatenate([rows[:, None] * inv, cols[:, None] * inv], axis=-1)
    return jnp.cos(ang), jnp.sin(ang)


def apply_rope(x, cos, sin):
    xf = x.astype(jnp.float32).reshape(x.shape[:-1] + (HEAD_DIM // 2, 2))
    x0, x1 = xf[..., 0], xf[..., 1]
    cc = cos[None, :, None, :]
    ss = sin[None, :, None, :]
    out = jnp.stack([x0 * cc - x1 * ss, x0 * ss + x1 * cc], axis=-1)
    return out.reshape(x.shape).astype(x.dtype)


def attend(q, k, v, mask=None, sink=None):
    s = jnp.einsum('bqhgd,bkhd->bhgqk', q, k, preferred_element_type=jnp.float32) * (HEAD_DIM ** -0.5)
    if mask is not None:
        s = jnp.where(mask, s, NEG)
    if sink is not None:
        sk = jnp.broadcast_to(sink.astype(jnp.float32)[None, :, :, None, None], s.shape[:-1] + (1,))
        p = jax.nn.softmax(jnp.concatenate([s, sk], axis=-1), axis=-1)[..., :-1]
    else:
        p = jax.nn.softmax(s, axis=-1)
    return jnp.einsum('bhgqk,bkhd->bqhgd', p.astype(v.dtype), v)


def dense_block_attention(q, k, v, sink=None):
    b, t, hq, dh = q.shape
    hkv = k.shape[2]
    g = hq // hkv
    qb = q.reshape(b, t // Q_BLOCK, Q_BLOCK, hkv, g, dh).transpose(1, 0, 2, 3, 4, 5)
    out = lax.map(lambda qi: attend(qi, k, v, None, sink), qb)
    return out.transpose(1, 0, 2, 3, 4, 5).reshape(b, t, hq * dh)


def window_attention_with_context(q, k, v, k_ctx, v_ctx, sink):
    b, t, hq, dh = q.shape
    hkv = k.shape[2]
    g = hq // hkv
    nb = t // Q_BLOCK
    span = Q_BLOCK + 2 * WINDOW
    n_ctx = k_ctx.shape[1]
    pad = ((0, 0), (WINDOW, WINDOW), (0, 0), (0, 0))
    kp = jnp.pad(k, pad)
    vp = jnp.pad(v, pad)
    qb = q.reshape(b, nb, Q_BLOCK, hkv, g, dh).transpose(1, 0, 2, 3, 4, 5)
    ctx_mask = jnp.ones((Q_BLOCK, n_ctx), dtype=bool)

    def one_block(args):
        qi, bi = args
        start = bi * Q_BLOCK
        kw = lax.dynamic_slice_in_dim(kp, start, span, axis=1)
        vw = lax.dynamic_slice_in_dim(vp, start, span, axis=1)
        qpos = start + jnp.arange(Q_BLOCK)
        kpos = start - WINDOW + jnp.arange(span)
        local = (jnp.abs(kpos[None, :] - qpos[:, None]) <= WINDOW) & (kpos[None, :] >= 0) & (kpos[None, :] < t)
        mask = jnp.concatenate([local, ctx_mask], axis=1)
        return attend(qi, jnp.concatenate([kw, k_ctx], axis=1), jnp.concatenate([vw, v_ctx], axis=1), mask, sink)

    out = lax.map(one_block, (qb, jnp.arange(nb)))
    return out.transpose(1, 0, 2, 3, 4, 5).reshape(b, t, hq * dh)


def chunk_spatial_gate(u, v, v_gain, w_s, b_s):
    b, t, _ = u.shape
    vn = rms_norm(v, v_gain).reshape(b, t // CHUNK, CHUNK, B_GROUPS, B_GROUP_DIM)
    mixed = jnp.einsum('gpq,bnqgc->bnpgc', w_s, vn) + b_s.T[None, None, :, :, None]
    return u * mixed.reshape(b, t, B_WIDTH)


def even_mixer(h, w_in, q_gain, k_gain, v_gain, w_s, b_s, w_out, rope=None, ctx_kv=None):
    b, t, _ = h.shape
    z = h @ w_in
    q, k, v, u_b, v_b = jnp.split(z, EVEN_SPLITS, axis=-1)
    q = rms_norm(q.reshape(b, t, A_HEADS, HEAD_DIM), q_gain)
    k = rms_norm(k.reshape(b, t, A_KV_HEADS, HEAD_DIM), k_gain)
    v = v.reshape(b, t, A_KV_HEADS, HEAD_DIM)
    if rope is not None:
        q = apply_rope(q, rope[0], rope[1])
        k_rot = apply_rope(k, rope[0], rope[1])
    else:
        k_rot = k
    if ctx_kv is not None:
        k_all = jnp.concatenate([k_rot, ctx_kv[0]], axis=1)
        v_all = jnp.concatenate([v, ctx_kv[1]], axis=1)
    else:
        k_all, v_all = k_rot, v
    attn = dense_block_attention(q, k_all, v_all)
    gate = chunk_spatial_gate(jax.nn.gelu(u_b), jax.nn.gelu(v_b), v_gain, w_s, b_s)
    return jnp.concatenate([attn, gate], axis=-1) @ w_out, k, v


def odd_mixer(h, w_in, q_gain, k_gain, sink, w_out, rope=None, ctx_kv=None):
    b, t, _ = h.shape
    z = h @ w_in
    q, k, v = jnp.split(z, ODD_SPLITS, axis=-1)
    q = rms_norm(q.reshape(b, t, C_HEADS, HEAD_DIM), q_gain)
    k = rms_norm(k.reshape(b, t, C_KV_HEADS, HEAD_DIM), k_gain)
    v = v.reshape(b, t, C_KV_HEADS, HEAD_DIM)
    sink_g = sink.reshape(C_KV_HEADS, C_HEADS // C_KV_HEADS)
    if ctx_kv is None:
        attn = dense_block_attention(q, k, v, sink_g)
    else:
        q = apply_rope(q, rope[0], rope[1])
        k_rot = apply_rope(k, rope[0], rope[1])
        attn = window_attention_with_context(q, k_rot, v, ctx_kv[0], ctx_kv[1], sink_g)
    return attn @ w_out, k, v


def conv_ffn(h, w_up, conv_w, conv_b, w_down):
    t = h.shape[1]
    z = h @ w_up
    half = CONV_W // 2
    zp = jnp.pad(z, ((0, 0), (half, half), (0, 0)))
    zc = conv_b + sum(zp[:, i:i + t] * conv_w[i] for i in range(CONV_W))
    a, g = jnp.split(zc, 2, axis=-1)
    return (a * jax.nn.silu(g)) @ w_down


def setup_inputs(seed: int = 0) -> dict:
    key = jax.random.key(seed)
    ks = jax.random.split(key, 32)
    f32 = jnp.float32
    nrm = lambda k, s, sc: jax.random.normal(k, s, f32) * sc
    return {
        "x_prompt": nrm(ks[0], (BATCH, SEQ, D_MODEL), 1.0),
        "x_sample": nrm(ks[1], (DEC_BATCH, DEC_SEQ, D_MODEL), 1.0),
        "cache_k_attn": nrm(ks[2], (DEC_BATCH, N_EVEN, PAST_LEN, A_KV_HEADS, HEAD_DIM), 1.0),
        "cache_v_attn": nrm(ks[3], (DEC_BATCH, N_EVEN, PAST_LEN, A_KV_HEADS, HEAD_DIM), 1.0),
        "cache_k_swa": nrm(ks[4], (DEC_BATCH, N_ODD, PAST_LEN, C_KV_HEADS, HEAD_DIM), 1.0),
        "cache_v_swa": nrm(ks[5], (DEC_BATCH, N_ODD, PAST_LEN, C_KV_HEADS, HEAD_DIM), 1.0),
        "c": nrm(ks[6], (DEC_BATCH, D_MODEL), 1.0),
        "c_ctx": nrm(ks[7], (D_MODEL,), 1.0),
        "w_mod": nrm(ks[8], (DEPTH, D_MODEL, 6 * D_MODEL), 0.5 * D_MODEL ** -0.5),
        "b_mod": nrm(ks[9], (DEPTH, 6 * D_MODEL), 0.02),
        "w_in_even": nrm(ks[10], (N_EVEN, D_MODEL, EVEN_IN), D_MODEL ** -0.5),
        "q_norm_a": 1.0 + nrm(ks[11], (N_EVEN, HEAD_DIM), 0.02),
        "k_norm_a": 1.0 + nrm(ks[12], (N_EVEN, HEAD_DIM), 0.02),
        "v_norm_b": 1.0 + nrm(ks[13], (N_EVEN, B_WIDTH), 0.02),
        "w_spatial": nrm(ks[14], (N_EVEN, B_GROUPS, CHUNK, CHUNK), CHUNK ** -0.5),
        "b_spatial": 1.0 + nrm(ks[15], (N_EVEN, B_GROUPS, CHUNK), 0.02),
        "w_out_even": nrm(ks[16], (N_EVEN, EVEN_MIX, D_MODEL), EVEN_MIX ** -0.5),
        "w_in_odd": nrm(ks[17], (N_ODD, D_MODEL, ODD_IN), D_MODEL ** -0.5),
        "q_norm_c": 1.0 + nrm(ks[18], (N_ODD, HEAD_DIM), 0.02),
        "k_norm_c": 1.0 + nrm(ks[19], (N_ODD, HEAD_DIM), 0.02),
        "sink_c": nrm(ks[20], (N_ODD, C_HEADS), 0.5),
        "w_out_odd": nrm(ks[21], (N_ODD, ODD_MIX, D_MODEL), ODD_MIX ** -0.5),
        "w_up": nrm(ks[22], (DEPTH, D_MODEL, 2 * FFN_DIM), D_MODEL ** -0.5),
        "conv_w": nrm(ks[23], (DEPTH, CONV_W, 2 * FFN_DIM), CONV_W ** -0.5),
        "conv_b": nrm(ks[24], (DEPTH, 2 * FFN_DIM), 0.02),
        "w_down": nrm(ks[25], (DEPTH, FFN_DIM, D_MODEL), FFN_DIM ** -0.5),
    }


def reference(x_prompt, x_sample, cache_k_attn, cache_v_attn, cache_k_swa, cache_v_swa, c, c_ctx,
              w_mod, b_mod, w_in_even, q_norm_a, k_norm_a, v_norm_b, w_spatial, b_spatial, w_out_even,
              w_in_odd, q_norm_c, k_norm_c, sink_c, w_out_odd, w_up, conv_w, conv_b, w_down):
    cos, sin = axial_rope(x_sample.shape[1])
    xc, xs = x_prompt, x_sample
    ks_a, vs_a, ks_c, vs_c = [], [], [], []
    for layer in range(DEPTH):
        j = layer // 2
        mod_c = modulation(c_ctx[None, :], w_mod[layer], b_mod[layer])
        mod_s = modulation(c, w_mod[layer], b_mod[layer])
        hc = adaln(xc, mod_c[0], mod_c[1])
        hs = adaln(xs, mod_s[0], mod_s[1])
        if layer % 2 == 0:
            out_c, kc, vc = even_mixer(hc, w_in_even[j], q_norm_a[j], k_norm_a[j], v_norm_b[j],
                                       w_spatial[j], b_spatial[j], w_out_even[j])
            ks_a.append(kc)
            vs_a.append(vc)
            out_s, _, _ = even_mixer(hs, w_in_even[j], q_norm_a[j], k_norm_a[j], v_norm_b[j],
                                     w_spatial[j], b_spatial[j], w_out_even[j],
                                     rope=(cos, sin), ctx_kv=(cache_k_attn[:, j], cache_v_attn[:, j]))
        else:
            out_c, kc, vc = odd_mixer(hc, w_in_odd[j], q_norm_c[j], k_norm_c[j], sink_c[j], w_out_odd[j])
            ks_c.append(kc)
            vs_c.append(vc)
            out_s, _, _ = odd_mixer(hs, w_in_odd[j], q_norm_c[j], k_norm_c[j], sink_c[j], w_out_odd[j],
                                    rope=(cos, sin), ctx_kv=(cache_k_swa[:, j], cache_v_swa[:, j]))
        xc = xc + mod_c[2] * out_c
        xs = xs + mod_s[2] * out_s
        xc = xc + mod_c[5] * conv_ffn(adaln(xc, mod_c[3], mod_c[4]), w_up[layer], conv_w[layer], conv_b[layer], w_down[layer])
        xs = xs + mod_s[5] * conv_ffn(adaln(xs, mod_s[3], mod_s[4]), w_up[layer], conv_w[layer], conv_b[layer], w_down[layer])
    new_k_attn = jnp.stack(ks_a, axis=1)
    new_v_attn = jnp.stack(vs_a, axis=1)
    new_k_swa = jnp.stack(ks_c, axis=1)
    new_v_swa = jnp.stack(vs_c, axis=1)
    return (xc, xs, new_k_attn, new_v_attn, new_k_swa, new_v_swa)
```

```python
import os
import numpy as np
import concourse.bass as bass
import concourse.mybir as mybir
from concourse.bass_utils import run_bass_kernel_spmd

F32 = mybir.dt.float32
BF16 = mybir.dt.bfloat16
I32 = mybir.dt.int32
U8 = mybir.dt.uint8
AF = mybir.ActivationFunctionType
ALU = mybir.AluOpType

N_CORES = 8
D = 1024
KC = 8
TS = 2048
TP = 1024
FF = 2816
FC = 22
EPS = 1e-6
LN_THETA = float(np.log(10000.0))


class Buf:
    __slots__ = ("w", "r", "t")

    def __init__(self):
        self.w = None
        self.r = {}
        self.t = 0


class Eng:
    def __init__(self, nc, h, name):
        self.h = h
        self.name = name
        self.sem = nc.alloc_semaphore("es_" + name)
        self.count = 0
        self.known = {}


class DSem:
    def __init__(self, nc, name):
        self.name = name
        self.sem = nc.alloc_semaphore("ds_" + name)
        self.count = 0


class K:
    def __init__(self, nc):
        self.nc = nc
        self.pe = Eng(nc, nc.tensor, "pe")
        self.act = Eng(nc, nc.scalar, "act")
        self.dve = Eng(nc, nc.vector, "dve")
        self.pool = Eng(nc, nc.gpsimd, "pool")
        self.sp = Eng(nc, nc.sync, "sp")
        self.engs = [self.pe, self.act, self.dve, self.pool, self.sp]
        self.sems = {e.name: e for e in self.engs}
        self.dsems = {}
        self.nops = 0
        self.clock = 0
        self.after_barrier = None
        self.snap = {}

    def dsem(self, name):
        d = DSem(self.nc, name)
        self.dsems[name] = d
        self.sems[name] = d
        return d

    def _waits(self, eng, reads, writes):
        deps = {}
        for b in reads:
            if b.w is not None:
                k, v = b.w
                if deps.get(k, 0) < v:
                    deps[k] = v
        for b in writes:
            if b.w is not None:
                k, v = b.w
                if deps.get(k, 0) < v:
                    deps[k] = v
            for k, v in b.r.items():
                if deps.get(k, 0) < v:
                    deps[k] = v
        for k, v in sorted(deps.items(), key=lambda kv: -kv[1]):
            if eng is self.pe and k == eng.name:
                continue
            if eng.known.get(k, 0) < v:
                eng.h.wait_ge(self.sems[k].sem, v)
                eng.known[k] = v
                for kk, vv in self.snap.get((k, v), ()):
                    if eng.known.get(kk, 0) < vv:
                        eng.known[kk] = vv

    def _commit(self, ev, reads, writes):
        k, v = ev
        self.clock += 1
        for b in writes:
            b.w = ev
            b.r = {}
            b.t = self.clock
        for b in reads:
            if b.r.get(k, 0) < v:
                b.r[k] = v
            b.t = self.clock

    def op(self, eng, fn, reads=(), writes=()):
        self._waits(eng, reads, writes)
        inst = fn()
        eng.count += 1
        inst.then_inc(eng.sem, 1)
        self.nops += 1
        self.snap[(eng.name, eng.count)] = tuple(eng.known.items())
        self._commit((eng.name, eng.count), [b for b in reads if b not in writes], writes)

    def dma(self, q, ds, pieces, reads=(), writes=()):
        self._waits(q, reads, writes)
        if ds.count and q.known.get(ds.name, 0) < ds.count:
            q.h.wait_ge(ds.sem, ds.count)
            q.known[ds.name] = ds.count
        for (o, i, kw) in pieces:
            q.h.dma_start(out=o, in_=i, **kw).then_inc(ds.sem, 16)
            ds.count += 16
        self.snap[(ds.name, ds.count)] = tuple(q.known.items())
        self._commit((ds.name, ds.count), [b for b in reads if b not in writes], writes)

    def barrier(self, engs=None):
        for e in (engs or [self.pe, self.act, self.dve, self.sp]):
            for name, s in self.sems.items():
                if s.count and e.known.get(name, 0) < s.count and name != e.name:
                    e.h.wait_ge(s.sem, s.count)
                    e.known[name] = s.count
            if e.count:
                e.h.wait_ge(e.sem, e.count)
        if self.after_barrier is not None:
            self.after_barrier()

    def final(self):
        e = self.sp
        for name, s in self.sems.items():
            if s.count and name != e.name:
                e.h.wait_ge(s.sem, s.count)


class Arena:
    def __init__(self, nc, nbytes):
        self.nc = nc
        nc.alloc_sbuf_tensor("arena", [128, nbytes], U8)
        self.base = None
        for a in list(nc.allocations):
            if getattr(a, "name", "") == "arena_set":
                self.base = a.memorylocations[0].addr
        assert self.base is not None
        self.size = nbytes
        self.top = 0
        self.hi = nbytes
        self.n = 0

    def alloc_hi(self, name, shape, dtype):
        esz = {F32: 4, BF16: 2, I32: 4}[dtype]
        nb = esz * int(np.prod(shape[1:]))
        nb = (nb + 31) // 32 * 32
        self.hi -= nb
        assert self.hi >= self.top, f"SBUF arena overflow (hi) at {name}"
        t = self.nc.alloc_sbuf_tensor_at(f"{name}_{self.n}", list(shape), dtype, offset=self.base + self.hi)
        self.n += 1
        return t

    def alloc(self, name, shape, dtype):
        esz = {F32: 4, BF16: 2, I32: 4}[dtype]
        nb = esz * int(np.prod(shape[1:]))
        nb = (nb + 31) // 32 * 32
        assert self.top + nb <= self.hi, f"SBUF arena overflow at {name}: {self.top}+{nb}>{self.hi}"
        t = self.nc.alloc_sbuf_tensor_at(f"{name}_{self.n}", list(shape), dtype, offset=self.base + self.top)
        self.n += 1
        self.top += nb
        return t


class _Stop(Exception):
    pass


def build(dbg=False, stop=None):
    nc = bass.Bass("TRN2", target_bir_lowering=False)
    k = K(nc)
    PE, ACT, DVE, POOL, SP = k.pe, k.act, k.dve, k.pool, k.sp

    def din(name, shape):
        return nc.dram_tensor(name, list(shape), F32, kind="ExternalInput").ap()

    def dout(name, shape):
        return nc.dram_tensor(name, list(shape), F32, kind="ExternalOutput").ap()

    xs = din("xs", [TS, D])
    xp = din("xp", [TP, D])
    cka = din("cka", [512, 128])
    cva = din("cva", [512, 128])
    cks = din("cks", [512, 256])
    cvs = din("cvs", [512, 256])
    cvec = din("cvec", [2, D])
    w_mod = din("w_mod", [2, D, 6 * D])
    b_mod = din("b_mod", [2, 6 * D])
    w_in_even = din("w_in_even", [D, 1792])
    q_norm_a = din("q_norm_a", [1, 64])
    k_norm_a = din("k_norm_a", [1, 64])
    v_norm_b = din("v_norm_b", [1, 512])
    w_spatial = din("w_spatial", [4, 128, 128])
    b_spatial = din("b_spatial", [1, 512])
    w_out_even = din("w_out_even", [D, D])
    w_in_odd = din("w_in_odd", [D, 1536])
    q_norm_c = din("q_norm_c", [1, 64])
    k_norm_c = din("k_norm_c", [1, 64])
    sink_c = din("sink_c", [1, 16])
    w_out_odd = din("w_out_odd", [D, D])
    w_up = din("w_up", [2, D, 2 * FF])
    conv_w = din("conv_w", [2, 3, 2 * FF])
    conv_b = din("conv_b", [2, 2 * FF])
    w_down = din("w_down", [2, FF, D])

    ys = dout("ys", [TS, D])
    yp = dout("yp", [TP, D])
    nka = dout("nka", [TP, 128])
    nva = dout("nva", [TP, 128])
    nks = dout("nks", [TP, 256])
    nvs = dout("nvs", [TP, 256])

    ar = Arena(nc, 208000)
    xT = ar.alloc("xT", [128, KC, TS], F32)
    ropeC = ar.alloc("ropeC", [128, TS], F32)
    ropeS = ar.alloc("ropeS", [128, TS], F32)
    NSLOT = 4
    ring = [ar.alloc(f"ring{i}", [128, 4096], BF16) for i in range(NSLOT)]
    ident = ar.alloc("ident", [128, 128], F32)
    rmat = ar.alloc("rmat", [128, 128], F32)
    bones = ar.alloc("bones", [128, 128], BF16)
    ones = ar.alloc("ones", [128, 128], BF16)
    maskA = ar.alloc("maskA", [128, 128], BF16)
    maskB = ar.alloc("maskB", [128, 128], BF16)
    ident_bf = ar.alloc("ident_bf", [128, 128], BF16)
    modT = ar.alloc("modT", [128, 2, 2, 48], F32)
    cwT = ar.alloc("cwT", [128, 2, 4, 44], F32)
    gains = ar.alloc("gains", [128, 4], F32)
    negB = ar.alloc("negB", [128, 2], F32)
    esink = ar.alloc("esink", [128, 16], F32)
    vgain = ar.alloc("vgain", [128, 512], F32)
    bsbc = ar.alloc("bsbc", [128, 512], F32)
    wsT = ar.alloc("wsT", [128, 4, 128], BF16)
    scT = ar.alloc("scT", [128, KC, 2], BF16)
    bmT = ar.alloc("bmT", [128, 2, 48], F32)
    halo = ar.alloc("halo", [128, KC, 2], BF16)
    b_halo = Buf()
    ptok = ar.alloc("ptok", [128, 8], F32)
    b_phase = Buf()
    k.after_barrier = lambda: k.op(DVE, lambda: nc.vector.memset(ptok[:, 0:1], 0.0), [], [b_phase])
    UNION = ar.top

    b_xT = [[Buf() for _ in range(TS // 128)] for _ in range(KC)]
    b_ring = [Buf() for _ in range(NSLOT)]
    ds_ring = [k.dsem(f"ring{i}") for i in range(NSLOT)]
    b_const = Buf()
    b_modL = [Buf(), Buf()]
    ds_setup = k.dsem("setup")
    ds_ld = [k.dsem("ld0"), k.dsem("ld1")]
    ds_st = [k.dsem("st0"), k.dsem("st1")]
    ds_kv = [k.dsem("kv0"), k.dsem("kv1")]
    ds_cache = k.dsem("cache")

    def xb(chunks, t0, t1):
        return [b_xT[c][g] for c in chunks for g in range(t0 // 128, (t1 + 127) // 128)]

    psall = nc.alloc_psum_tensor("psall", [128, 8, 512], F32)
    psum = [psall[:, i, :] for i in range(8)]
    b_ps = [Buf() for _ in range(8)]
    pa_state = [0]
    pb_state = [0]

    def bankA():
        i = min(range(0, 5), key=lambda j: b_ps[j].t)
        k.clock += 1
        b_ps[i].t = k.clock
        return psum[i], b_ps[i]

    def bankPair():
        b0 = min((0, 2), key=lambda j: max(b_ps[j].t, b_ps[j + 1].t))
        k.clock += 1
        b_ps[b0].t = k.clock
        b_ps[b0 + 1].t = k.clock
        return b0

    def bankB():
        i = min(range(5, 8), key=lambda j: b_ps[j].t)
        k.clock += 1
        b_ps[i].t = k.clock
        return psum[i], b_ps[i]

    wq = []
    wq_pos = [0]

    class WL:
        def __init__(self, fn):
            self.fn = fn
            self.slot = None
            self.bs = None
            self.idx = len(wq)
            wq.append(self)

    def wget(wl, ahead=2):
        tgt = min(len(wq), wl.idx + 1 + ahead)
        while wq_pos[0] < tgt:
            w = wq[wq_pos[0]]
            slot, bs, dss = ring_next()
            w.fn(slot, bs, dss)
            w.slot, w.bs = slot, bs
            wq_pos[0] += 1
        return wl.slot, wl.bs

    def run_pipe(gens):
        active = []
        it = iter(gens)
        more = True
        while more or active:
            g = next(it, None) if more else None
            if g is None:
                more = False
            else:
                try:
                    next(g)
                    active.append(g)
                except StopIteration:
                    pass
            for a in list(active):
                if a is g:
                    continue
                try:
                    next(a)
                except StopIteration:
                    active.remove(a)

    ring_state = [0]

    def ring_next():
        i = ring_state[0] % NSLOT
        ring_state[0] += 1
        return ring[i], b_ring[i], ds_ring[i]

    ds_dbg = k.dsem("dbg")
    dbg_names = set(dbg) if dbg else set()

    def dump(name, t, shape, dtype):
        if name not in dbg_names:
            return
        dt_ = nc.dram_tensor("dbg_" + name, list(shape), dtype, kind="ExternalOutput").ap()
        k.barrier()
        k.dma(SP, ds_dbg, [(dt_, t, {})], [], [])
        SP.h.wait_ge(ds_dbg.sem, ds_dbg.count)
        SP.known[ds_dbg.name] = ds_dbg.count
        k.barrier()

    def act_op(out, in_, func, reads, writes, bias=None, scale=None, accum_out=None):
        kw = {}
        if bias is not None:
            kw["bias"] = bias
        if scale is not None:
            kw["scale"] = scale
        if accum_out is not None:
            kw["accum_out"] = accum_out
        k.op(ACT, lambda: nc.scalar.activation(out=out, in_=in_, func=func, **kw), reads, writes)

    def mm_group(out, pairs, reads, writes):
        def fn():
            inst = None
            n = len(pairs)
            for i, (l, r) in enumerate(pairs):
                inst = nc.tensor.matmul(out, lhsT=l, rhs=r, start=(i == 0), stop=(i == n - 1))
            return inst
        k.op(PE, fn, reads, writes)

    def transpose_to(out_ps, in_sb, reads, writes, kp=128):
        k.op(PE, lambda: nc.tensor.transpose(out_ps, in_sb, ident[0:kp, 0:kp]), reads + [b_const], writes)

    x_stage = None
    def load_x(src, T):
        for _ in load_x_gen(src, T):
            pass

    def load_x_gen(src, T):
        xT4 = xT[:].bitcast(BF16).rearrange("p c (n two) -> p c n two", two=2)
        for t in range(T // 128):
            sl = t % 2
            st_t, b_st = x_stage[sl]
            k.dma(SP, ds_ld[sl], [(st_t[:], src[t * 128:(t + 1) * 128, :], {})], [], [b_st])
            st4 = st_t[:].bitcast(BF16).rearrange("p (n two) -> p n two", two=2)
            for half in range(2):
                pt, bpt = bankA()
                pb = pt.bitcast(BF16)
                for cc in range(4):
                    c = half * 4 + cc
                    for hl in range(2):
                        r = hl * 4 + cc
                        k.op(PE, lambda: nc.tensor.transpose(pb[:, r * 128:(r + 1) * 128], st4[:, c * 128:(c + 1) * 128, hl], ident_bf[:]), [b_st, b_const], [bpt])
                wb = [b_xT[c][t] for c in range(half * 4, half * 4 + 4)]
                for hl in range(2):
                    dstv = xT4[:, half * 4:half * 4 + 4, t * 128:(t + 1) * 128, hl]
                    srcv = pb[:, hl * 512:(hl + 1) * 512].rearrange("p (c n) -> p c n", c=4)
                    k.op(DVE, lambda: nc.vector.tensor_copy(out=dstv, in_=srcv), [bpt], wb)
            yield

    def store_x(dst, T):
        xT4 = xT[:].bitcast(BF16).rearrange("p c (n two) -> p c n two", two=2)
        for t in range(T // 128):
            sl = t % 2
            st_t, b_st = x_stage[sl]
            st4 = st_t[:].bitcast(BF16).rearrange("p (n two) -> p n two", two=2)
            for half in range(2):
                pt, bpt = bankA()
                pb = pt.bitcast(BF16)
                for cc in range(4):
                    c = half * 4 + cc
                    for hl in range(2):
                        r = hl * 4 + cc
                        k.op(PE, lambda: nc.tensor.transpose(pb[:, r * 128:(r + 1) * 128], xT4[:, c, t * 128:(t + 1) * 128, hl], ident_bf[:]), [b_xT[c][t], b_const], [bpt])
                for hl in range(2):
                    k.op(DVE, lambda: nc.vector.tensor_copy(out=st4[:, half * 512:(half + 1) * 512, hl], in_=pb[:, hl * 512:(hl + 1) * 512]), [bpt], [b_st])
            k.dma(SP, ds_st[sl], [(dst[t * 128:(t + 1) * 128, :], st_t[:], {})], [b_st], [])

    tmp_i = ar.alloc("tmp_i", [128, 128], I32)
    tmp_f = ar.alloc("tmp_f", [128, 128], F32)
    tmp_g = ar.alloc("tmp_g", [128, 2048], F32)
    tmp_h = ar.alloc("tmp_h", [128, 2048], F32)
    tmp_s = ar.alloc("tmp_s", [128, 8], F32)
    stg = ar.alloc("stg", [128, 1024], F32)
    b_ti, b_tf, b_tg, b_th, b_ts, b_stg = Buf(), Buf(), Buf(), Buf(), Buf(), Buf()

    k.op(POOL, lambda: nc.gpsimd.iota(tmp_i[:], pattern=[[1, 128]], base=0, channel_multiplier=-1), [], [b_ti])
    k.op(DVE, lambda: nc.vector.tensor_copy(out=tmp_f[:], in_=tmp_i[:]), [b_ti], [b_tf])
    k.op(DVE, lambda: nc.vector.tensor_single_scalar(out=ident[:], in_=tmp_f[:], scalar=0.0, op=ALU.is_equal), [b_tf], [b_const])
    k.op(DVE, lambda: nc.vector.tensor_scalar(out=maskA[:], in0=tmp_f[:], scalar1=0.0, scalar2=-30000.0, op0=ALU.is_gt, op1=ALU.mult), [b_tf], [b_const])
    k.op(DVE, lambda: nc.vector.tensor_scalar(out=maskB[:], in0=tmp_f[:], scalar1=0.0, scalar2=-30000.0, op0=ALU.is_lt, op1=ALU.mult), [b_tf], [b_const])
    k.op(DVE, lambda: nc.vector.tensor_single_scalar(out=ident_bf[:], in_=tmp_f[:], scalar=0.0, op=ALU.is_equal), [b_tf], [b_const])
    k.op(DVE, lambda: nc.vector.memset(ones[:], 1.0), [], [b_const])
    k.op(DVE, lambda: nc.vector.memset(bones[:], 0.0), [], [b_const])
    k.op(DVE, lambda: nc.vector.memset(bones[0:64, 0:64], 1.0), [], [b_const])
    k.op(DVE, lambda: nc.vector.memset(bones[64:128, 64:128], 1.0), [], [b_const])
    tmp_pi = ar.alloc("tmp_pi", [128, 4], I32)
    b_pi = Buf()
    k.op(POOL, lambda: nc.gpsimd.iota(tmp_pi[:, 0:1], pattern=[[0, 1]], base=0, channel_multiplier=1), [], [b_pi])
    k.op(DVE, lambda: nc.vector.tensor_single_scalar(out=tmp_pi[:, 1:2], in_=tmp_pi[:, 0:1], scalar=1, op=ALU.bitwise_and), [b_pi], [b_pi])
    k.op(DVE, lambda: nc.vector.tensor_scalar(out=tmp_pi[:, 2:3], in0=tmp_pi[:, 0:1], scalar1=5, scalar2=1, op0=ALU.arith_shift_right, op1=ALU.bitwise_and), [b_pi], [b_pi])
    k.op(DVE, lambda: nc.vector.tensor_scalar(out=tmp_pi[:, 3:4], in0=tmp_pi[:, 0:1], scalar1=1, scalar2=15, op0=ALU.arith_shift_right, op1=ALU.bitwise_and), [b_pi], [b_pi])
    k.op(DVE, lambda: nc.vector.tensor_copy(out=tmp_s[:, 0:4], in_=tmp_pi[:, 0:4]), [b_pi], [b_ts])
    act_op(tmp_s[:, 4:5], tmp_s[:, 3:4], AF.Exp, [b_ts], [b_ts], scale=-LN_THETA / 16.0)
    k.op(DVE, lambda: nc.vector.tensor_scalar(out=tmp_s[:, 5:6], in0=tmp_s[:, 1:2], scalar1=-1.0, scalar2=1.0, op0=ALU.mult, op1=ALU.add), [b_ts], [b_ts])
    tmp_f2 = ar.alloc("tmp_f2", [128, 128], F32)
    b_tf2 = Buf()
    k.op(DVE, lambda: nc.vector.tensor_scalar(out=rmat[:], in0=tmp_f[:], scalar1=1.0, scalar2=tmp_s[:, 5:6], op0=ALU.is_equal, op1=ALU.mult), [b_tf, b_ts], [b_const])
    k.op(DVE, lambda: nc.vector.tensor_scalar(out=tmp_f2[:], in0=tmp_f[:], scalar1=-1.0, scalar2=tmp_s[:, 1:2], op0=ALU.is_equal, op1=ALU.mult), [b_tf, b_ts], [b_tf2])
    k.op(DVE, lambda: nc.vector.tensor_tensor(out=rmat[:], in0=rmat[:], in1=tmp_f2[:], op=ALU.subtract), [b_tf2], [b_const])

    b_rope = Buf()
    b_par = Buf()
    def setup_dma(out, in_, q=SP, **kw):
        k.dma(q, ds_setup, [(out, in_, kw)], [], [b_par])

    for gi, gsrc in enumerate((q_norm_a, k_norm_a, q_norm_c, k_norm_c)):
        for h0 in (0, 64):
            setup_dma(gains[h0:h0 + 64, gi:gi + 1], gsrc.rearrange("o d -> d o"))
    setup_dma(vgain[:], v_norm_b.partition_broadcast(128))
    setup_dma(bsbc[:], b_spatial.partition_broadcast(128))
    setup_dma(esink[:], sink_c.partition_broadcast(128))
    gb = ar.alloc("gb", [128, 4, 64], F32)
    for gi, gsrc in enumerate((q_norm_a, k_norm_a, q_norm_c, k_norm_c)):
        setup_dma(gb[:, gi, :], gsrc.partition_broadcast(128))
    def load_T(src_rows, R, dst):
        k.dma(SP, ds_ld[0], [(stg[0:R, 0:128], src_rows, {})], [], [b_stg])
        pt, bpt = bankA()
        transpose_to(pt[:, 0:R], stg[0:R, 0:128], [b_stg], [bpt], kp=R)
        k.op(DVE, lambda: nc.vector.tensor_copy(out=dst, in_=pt[:, 0:R]), [bpt], [b_const])

    for L in range(2):
        for r in range(2):
            pass
    for L in range(2):
        load_T(b_mod[L].rearrange("(j p) -> j p", p=128), 48, bmT[:, L, :])
        for i in range(3):
            load_T(conv_w[L, i].rearrange("(j p) -> j p", p=128), 44, cwT[:, L, i, :])
        load_T(conv_b[L].rearrange("(j p) -> j p", p=128), 44, cwT[:, L, 3, :])
    cT = ar.alloc("cT", [128, 16], F32)
    load_T(cvec.rearrange("r (c p) -> (r c) p", p=128), 16, cT[:])
    act_op(scT[:].rearrange("p c r -> p r c"), cT[:].rearrange("p (r c) -> p r c", r=2), AF.Silu, [b_const], [b_const])
    for g in range(4):
        k.dma(SP, ds_ld[0], [(stg[:, 0:128], w_spatial[g], {})], [], [b_stg])
        pt, bpt = bankA()
        transpose_to(pt[:, 0:128], stg[:, 0:128], [b_stg], [bpt])
        k.op(DVE, lambda g=g, pt=pt: nc.vector.tensor_copy(out=wsT[:, g, :], in_=pt[:, 0:128]), [bpt], [b_const])

    def mod_loads(L, blks=range(12)):
        res = {}
        for blk in blks:
            def fn(slot, bs, dss, L=L, blk=blk):
                sv = slot[:].rearrange("p (c n) -> p c n", c=KC)
                k.dma(POOL, dss, [(sv, w_mod[L][:, blk * 512:(blk + 1) * 512].rearrange("(c p) n -> p c n", p=128), {})], [], [bs])
            res[blk] = WL(fn)
        return res

    def mod_block(L, blk, wl):
        slot, bs = wget(wl)
        sv = slot[:].rearrange("p (c n) -> p c n", c=KC)
        pt, bpt = bankA()
        for jj in range(4):
            mm_group(pt[:, jj * 2:jj * 2 + 2], [(sv[:, c, jj * 128:(jj + 1) * 128], scT[:, c, :]) for c in range(KC)], [bs, b_const], [bpt])
        for r in range(2):
            k.op(DVE, lambda: nc.vector.tensor_tensor(
                out=modT[:, L, r, blk * 4:blk * 4 + 4], in0=pt[:, 0:8].rearrange("p (j r) -> p r j", r=2)[:, r, :],
                in1=bmT[:, L, blk * 4:blk * 4 + 4], op=ALU.add), [bpt, b_const], [b_modL[L]])

    def mod_finish(L, j0s=(8, 32)):
        for r in range(2):
            for j0 in j0s:
                k.op(DVE, lambda: nc.vector.tensor_scalar(out=modT[:, L, r, j0:j0 + 8], in0=modT[:, L, r, j0:j0 + 8], scalar1=1.0, scalar2=None, op0=ALU.add), [], [b_modL[L]])

    k.op(POOL, lambda: nc.gpsimd.iota(tmp_g[:], pattern=[[1, 32], [0, 64]], base=0, channel_multiplier=0, allow_small_or_imprecise_dtypes=True), [], [b_tg])
    k.op(POOL, lambda: nc.gpsimd.iota(tmp_h[:], pattern=[[0, 32], [1, 64]], base=0, channel_multiplier=0, allow_small_or_imprecise_dtypes=True), [], [b_th])
    k.op(DVE, lambda: nc.vector.tensor_tensor(out=tmp_h[:], in0=tmp_h[:], in1=tmp_g[:], op=ALU.subtract), [b_tg], [b_th])
    k.op(DVE, lambda: nc.vector.scalar_tensor_tensor(out=tmp_g[:], in0=tmp_h[:], scalar=tmp_s[:, 2:3], in1=tmp_g[:], op0=ALU.mult, op1=ALU.add), [b_th, b_ts], [b_tg])
    k.op(DVE, lambda: nc.vector.tensor_scalar(out=tmp_g[:], in0=tmp_g[:], scalar1=tmp_s[:, 4:5], scalar2=None, op0=ALU.mult), [b_ts], [b_tg])
    TWO_PI = float(2 * np.pi)

    def sin_of(dst, shift):
        k.op(DVE, lambda: nc.vector.tensor_scalar(out=tmp_h[:], in0=tmp_g[:], scalar1=shift, scalar2=1.0 / TWO_PI, op0=ALU.add, op1=ALU.mult), [b_tg], [b_th])
        ti = ar_tmp_i2
        k.op(DVE, lambda: nc.vector.tensor_copy(out=ti[:], in_=tmp_h[:]), [b_th], [b_ti2])
        k.op(DVE, lambda: nc.vector.tensor_copy(out=tmp_h[:], in_=ti[:]), [b_ti2], [b_th])
        k.op(DVE, lambda: nc.vector.tensor_scalar(out=tmp_h[:], in0=tmp_h[:], scalar1=-TWO_PI, scalar2=shift, op0=ALU.mult, op1=ALU.add), [], [b_th])
        k.op(DVE, lambda: nc.vector.tensor_tensor(out=tmp_h[:], in0=tmp_h[:], in1=tmp_g[:], op=ALU.add), [b_tg], [b_th])
        for sgn in (1.0, -1.0):
            k.op(DVE, lambda sgn=sgn: nc.vector.tensor_scalar(out=dst, in0=tmp_h[:], scalar1=sgn, scalar2=float(np.pi), op0=ALU.mult, op1=ALU.is_gt), [b_th], [b_rope])
            k.op(DVE, lambda sgn=sgn: nc.vector.scalar_tensor_tensor(out=tmp_h[:], in0=dst, scalar=-sgn * TWO_PI, in1=tmp_h[:], op0=ALU.mult, op1=ALU.add), [b_rope], [b_th])
        k.op(DVE, lambda: nc.vector.tensor_scalar(out=tmp_h[:], in0=tmp_h[:], scalar1=float(np.pi), scalar2=-float(np.pi), op0=ALU.min, op1=ALU.max), [], [b_th])
        act_op(dst, tmp_h[:], AF.Sin, [b_th], [b_rope])

    ar_tmp_i2 = ar.alloc("tmp_i2", [128, 2048], I32)
    b_ti2 = Buf()
    x_stage = [(ar.alloc("xst", [128, 1024], F32), Buf()) for _ in range(2)]
    gx = load_x_gen(xs, TS)
    ml0 = mod_loads(0, range(6))
    sin_of(ropeS[:], 0.0)
    for blk in range(3):
        mod_block(0, blk, ml0[blk])
        next(gx, None)
        next(gx, None)
    sin_of(ropeC[:], float(np.pi / 2))
    for blk in range(3, 6):
        mod_block(0, blk, ml0[blk])
        next(gx, None)
        next(gx, None)
    for _ in gx:
        pass
    mod_finish(0, (8,))
    gmax = ar.alloc("gmax", [128, 4], F32)
    k.op(DVE, lambda: nc.vector.tensor_reduce(out=gmax[:], in_=gb[:], axis=mybir.AxisListType.X, op=ALU.max, apply_absolute_value=True), [b_par], [b_const])
    for L in range(2):
        k.op(DVE, lambda L=L: nc.vector.scalar_tensor_tensor(out=negB[:, L:L + 1], in0=gmax[:, 2 * L:2 * L + 1], scalar=-8.0, in1=gmax[:, 2 * L + 1:2 * L + 2], op0=ALU.mult, op1=ALU.mult), [b_const], [b_const])
    act_op(esink[:], esink[:], AF.Exp, [b_const], [b_const, b_par], bias=negB[:, 1:2])


    dump("modT", modT[:], [128, 2, 2, 48], F32)
    dump("ropeC", ropeC[:], [128, TS], F32)
    dump("ropeS", ropeS[:], [128, TS], F32)
    dump("rmat", rmat[:], [128, 128], F32)
    dump("cwT", cwT[:], [128, 2, 4, 44], F32)
    dump("negB", negB[:], [128, 2], F32)
    dump("esink", esink[:], [128, 16], F32)
    k.barrier()
    ar.top = UNION

    def new_sets(n):
        sets = [((ar.alloc("tB", [128, 512], BF16), Buf()), (ar.alloc("tF", [128, 512], F32), Buf()), (ar.alloc("tS", [128, 8], F32), Buf()))
                for _ in range(n)]
        st = [0]

        def nxt():
            i = st[0] % n
            st[0] += 1
            return sets[i]
        return nxt

    def adaln(L, r, sel, t0, n, dst, dst_col, dst_bufs, nxt):
        jsh, jsc = (0, 8) if sel == 0 else (24, 32)
        pss, bpss = bankA()
        bm = b_modL[L]
        for c in range(KC):
            (sq, bsq), _, _ = nxt()
            if c % 3 == 0:
                act_op(sq[:, 0:n], xT[:, c, t0:t0 + n], AF.Square, xb([c], t0, t0 + n), [bsq])
            elif c % 3 == 1:
                k.op(DVE, lambda: nc.vector.tensor_tensor(out=sq[:, 0:n], in0=xT[:, c, t0:t0 + n], in1=xT[:, c, t0:t0 + n], op=ALU.mult), xb([c], t0, t0 + n), [bsq])
            else:
                k.op(POOL, lambda: nc.gpsimd.tensor_tensor(out=sq[:, 0:n], in0=xT[:, c, t0:t0 + n], in1=xT[:, c, t0:t0 + n], op=ALU.mult), xb([c], t0, t0 + n) + [b_phase], [bsq])
            k.op(PE, lambda: nc.tensor.matmul(pss[:, 0:n], lhsT=ones[:], rhs=sq[:, 0:n], start=(c == 0), stop=(c == KC - 1)), [bsq, b_const], [bpss])
        act_op(pss[:, 0:n], pss[:, 0:n], AF.Ln, [], [bpss], bias=EPS, scale=1.0 / D)
        act_op(pss[:, 0:n], pss[:, 0:n], AF.Exp, [], [bpss], scale=-0.5)
        for c in range(KC):
            _, (tm, btm), _ = nxt()
            k.op(DVE, lambda: nc.vector.scalar_tensor_tensor(
                out=tm[:, 0:n], in0=xT[:, c, t0:t0 + n], scalar=modT[:, L, r, jsc + c:jsc + c + 1], in1=pss[:, 0:n],
                op0=ALU.mult, op1=ALU.mult), xb([c], t0, t0 + n) + [bpss, bm], [btm])
            act_op(dst[:, c, dst_col:dst_col + n], tm[:, 0:n], AF.Identity, [btm, bm], [dst_bufs[c]],
                   bias=modT[:, L, r, jsh + c:jsh + c + 1])

    def adaln_gen(L, r, sel, t0, n, dst, dst_col, dst_bufs, nxt):
        jsh, jsc = (0, 8) if sel == 0 else (24, 32)
        pss, bpss = bankB()
        bm = b_modL[L]
        for c in range(KC):
            (sq, bsq), _, _ = nxt()
            if c % 3 == 0:
                act_op(sq[:, 0:n], xT[:, c, t0:t0 + n], AF.Square, xb([c], t0, t0 + n), [bsq])
            elif c % 3 == 1:
                k.op(DVE, lambda: nc.vector.tensor_tensor(out=sq[:, 0:n], in0=xT[:, c, t0:t0 + n], in1=xT[:, c, t0:t0 + n], op=ALU.mult), xb([c], t0, t0 + n), [bsq])
            else:
                k.op(POOL, lambda: nc.gpsimd.tensor_tensor(out=sq[:, 0:n], in0=xT[:, c, t0:t0 + n], in1=xT[:, c, t0:t0 + n], op=ALU.mult), xb([c], t0, t0 + n) + [b_phase], [bsq])
            yield
            k.op(PE, lambda: nc.tensor.matmul(pss[:, 0:n], lhsT=ones[:], rhs=sq[:, 0:n], start=(c == 0), stop=(c == KC - 1)), [bsq, b_const], [bpss])
        act_op(pss[:, 0:n], pss[:, 0:n], AF.Ln, [], [bpss], bias=EPS, scale=1.0 / D)
        act_op(pss[:, 0:n], pss[:, 0:n], AF.Exp, [], [bpss], scale=-0.5)
        yield
        for c in range(KC):
            _, (tm, btm), _ = nxt()
            k.op(DVE, lambda: nc.vector.scalar_tensor_tensor(
                out=tm[:, 0:n], in0=xT[:, c, t0:t0 + n], scalar=modT[:, L, r, jsc + c:jsc + c + 1], in1=pss[:, 0:n],
                op0=ALU.mult, op1=ALU.mult), xb([c], t0, t0 + n) + [bpss, bm], [btm])
            act_op(dst[:, c, dst_col:dst_col + n], tm[:, 0:n], AF.Identity, [btm, bm], [dst_bufs[c]],
                   bias=modT[:, L, r, jsh + c:jsh + c + 1])
            if c % 2 == 1:
                yield

    def run_group(grp):
        S = grp == "S"
        T = TS if S else TP
        r = 0 if S else 1
        nblk = T // 512
        seqs = [(0, 2048)] if S else [(i * 256, 256) for i in range(4)]

        if stop == "setup":
            raise _Stop()
        mark = ar.top
        nonlocal x_stage
        if not S:
            x_stage = [(ar.alloc("xst", [128, 1024], F32), Buf()) for _ in range(2)]
            load_x(xp, T)
            k.barrier()
        ar.top = mark

        for L in range(2):
            even = L == 0
            bm = b_modL[L]
            NQC = 4 if even else 8
            NKC = 1 if even else 2
            NKV = 2 * NKC
            TK = T + 512 if S else T
            w_in = w_in_even if even else w_in_odd
            w_out = w_out_even if even else w_out_odd
            gq = gains[:, 0:1] if even else gains[:, 2:3]
            gk = gains[:, 1:2] if even else gains[:, 3:4]
            QOFF, KOFF = 0, (512 if even else 1024)
            WKV = 256 * NKC
            WV = 128 * NKC

            def q_load(m):
                def fn(slot, bs, dss):
                    sv = slot[:].rearrange("p (c n) -> p c n", c=KC)
                    k.dma(POOL, dss, [(sv[:, :, j * 128 + hf * 64:j * 128 + hf * 64 + 64],
                                       w_in[:, QOFF + m * 512 + hf * 256 + j * 64:QOFF + m * 512 + hf * 256 + j * 64 + 64].rearrange("(c p) d -> p c d", p=128), {})
                                      for hf in range(2) for j in range(4)], [], [bs])
                return WL(fn)

            def col_load(c0, w):
                def fn(slot, bs, dss):
                    sv = slot[:, 0:KC * w].rearrange("p (c n) -> p c n", c=KC)
                    k.dma(POOL, dss, [(sv, w_in[:, c0:c0 + w].rearrange("(c p) n -> p c n", p=128), {})], [], [bs])
                return WL(fn)

            def wout_load(cb):
                def fn(slot, bs, dss):
                    sv = slot[:].rearrange("p (c n) -> p c n", c=KC)
                    ncols = slice(cb * 512, (cb + 1) * 512)
                    pieces = []
                    for m in range(NQC // 4):
                        base = 512 * m
                        pieces.append((sv[0:64, 4 * m:4 * m + 4, :], w_out[base:base + 256, ncols].rearrange("(j d) n -> d j n", d=64), {}))
                        pieces.append((sv[64:128, 4 * m:4 * m + 4, :], w_out[base + 256:base + 512, ncols].rearrange("(j d) n -> d j n", d=64), {}))
                    if even:
                        pieces.append((sv[:, 4:8, :], w_out[512:1024, ncols].rearrange("(g p) n -> p g n", p=128), {}))
                    k.dma(POOL, dss, pieces, [], [bs])
                return WL(fn)

            def up_load(u):
                def fn(slot, bs, dss):
                    sv = slot[:].rearrange("p (c n) -> p c n", c=KC)
                    j0 = 2 * u
                    k.dma(POOL, dss, [
                        (sv[:, :, 0:256], w_up[L][:, j0 * 128:j0 * 128 + 256].rearrange("(c p) n -> p c n", p=128), {}),
                        (sv[:, :, 256:512], w_up[L][:, FF + j0 * 128:FF + j0 * 128 + 256].rearrange("(c p) n -> p c n", p=128), {}),
                    ], [], [bs])
                return WL(fn)

            def down_load(co):
                def fn(slot, bs, dss):
                    sv = slot[:, 0:FC * 128].rearrange("p (f n) -> p f n", f=FC)
                    k.dma(POOL, dss, [(sv, w_down[L][:, co * 128:(co + 1) * 128].rearrange("(f p) n -> p f n", p=128), {})], [], [bs])
                return WL(fn)

            wl_proj = []
            for blk in range(nblk):
                d_ = {"q": [q_load(m) for m in range(NQC // 4)], "kv": col_load(KOFF, WKV)}
                if even:
                    d_["u"] = col_load(768, 512)
                    d_["vb"] = col_load(1280, 512)
                wl_proj.append(d_)
            wl_mod0b = mod_loads(0, range(6, 12)) if (S and even) else {}
            wl_mod1 = mod_loads(1) if (S and even) else {}
            wl_out = [wout_load(cb) for cb in range(2)]
            nsb = 2 if S else 1
            wl_ffn = [{"u": [up_load(u) for u in range(FC // 2)], "d": [down_load(co) for co in range(KC)]} for _ in range(nsb)]

            mark = ar.top
            QA = ar.alloc("QA", [128, 8, T], BF16)
            mark_qa = ar.top
            KT = ar.alloc("KT", [128, NKC, TK], BF16)
            VA = ar.alloc("VA", [128, TK // 128, NKV, 128], BF16)
            b_QA = [[Buf() for _ in range(T // 128)] for _ in range(8)]
            b_KT = [[Buf() for _ in range(TK // 128)] for _ in range(NKC)]
            b_VA = [Buf() for _ in range(TK // 128)]
            mark2 = ar.top
            dbl = (not S) or even
            hTs = [ar.alloc("hT", [128, KC, 512], BF16) for _ in range(2 if dbl else 1)]
            b_hTs = [[Buf() for _ in range(KC)] for _ in hTs]
            hT, b_hT = hTs[0], b_hTs[0]
            if dbl:
                asets = [((ar.alloc("aB", [128, 512], BF16), Buf()), (ar.alloc("aF", [128, 512], F32), Buf()), (None, None)) for _ in range(2)]
                as_i = [0]

                def anxt():
                    i = as_i[0] % 2
                    as_i[0] += 1
                    return asets[i]
            kvst = [(ar.alloc("kvst", [128, 256], F32), Buf()) for _ in range(2)]
            nxt = new_sets(3)

            def qa_b(chunks, t0, t1):
                return [b_QA[c][g] for c in chunks for g in range(t0 // 128, (t1 + 127) // 128)]

            k.op(DVE, lambda: nc.vector.memset(VA[:, :, :, 64:128], 1.0), [], b_VA)

            if S:
                ck = cka if even else cks
                cv = cva if even else cvs
                for t in range(4):
                    st_t, b_st = kvst[t % 2]
                    k.dma(SP, ds_cache, [(st_t[:, 0:WV], ck[t * 128:(t + 1) * 128, :], {})], [], [b_st])
                    for m in range(NKC):
                        pt, bpt = bankA()
                        transpose_to(pt[:, 0:128], st_t[:, m * 128:(m + 1) * 128], [b_st], [bpt])
                        k.op(DVE, lambda: nc.vector.tensor_copy(out=KT[:, m, T + t * 128:T + (t + 1) * 128], in_=pt[:, 0:128]), [bpt], [b_KT[m][T // 128 + t]])
                for t in range(4):
                    st_t, b_st = kvst[t % 2]
                    k.dma(SP, ds_cache, [(st_t[:, 0:WV], cv[t * 128:(t + 1) * 128, :], {})], [], [b_st])
                    k.op(DVE, lambda: nc.vector.tensor_copy(
                        out=VA[:, T // 128 + t, :, 0:64], in_=st_t[:, 0:WV].rearrange("p (h d) -> p h d", d=64)), [b_st], [b_VA[T // 128 + t]])

            def g_qk(wl, col, gcol, dst, dbufs, t0, rope, kout):
                slot, bs = wget(wl)
                sv = slot[:].rearrange("p (c n) -> p c n", c=KC) if wl.kind == "q" else slot[:, 0:KC * WKV].rearrange("p (c n) -> p c n", c=KC)
                (sq, bsq), (qn, bqn), _ = nxt()
                ps, bps = bankA()
                mm_group(ps[:], [(sv[:, kc, col:col + 128], hT[:, kc, :]) for kc in range(KC)], [bs] + b_hT, [bps])
                act_op(sq[:], ps[:], AF.Square, [bps], [bsq])
                act_op(qn[:], ps[:], AF.Identity, [bps, b_const], [bqn], scale=gcol)
                yield
                pss, bpss = bankA()
                mm_group(pss[:], [(bones[:], sq[:])], [bsq, b_const], [bpss])
                if rope:
                    prot, bprot = bankA()
                    mm_group(prot[:], [(rmat[:], qn[:])], [bqn, b_const], [bprot])
                act_op(pss[:], pss[:], AF.Ln, [], [bpss], bias=EPS, scale=1.0 / 64)
                act_op(pss[:], pss[:], AF.Exp, [], [bpss], scale=-0.5)
                if rope:
                    k.op(DVE, lambda: nc.vector.tensor_tensor(out=prot[:], in0=prot[:], in1=ropeS[:, t0:t0 + 512], op=ALU.mult), [b_const], [bprot])
                    k.op(DVE, lambda: nc.vector.tensor_tensor(out=qn[:], in0=qn[:], in1=ropeC[:, t0:t0 + 512], op=ALU.mult), [b_const], [bqn])
                    k.op(DVE, lambda: nc.vector.tensor_tensor(out=qn[:], in0=qn[:], in1=prot[:], op=ALU.add), [bprot], [bqn])
                    k.op(DVE, lambda: nc.vector.tensor_tensor(out=dst, in0=qn[:], in1=pss[:], op=ALU.mult), [bqn, bpss], dbufs)
                    return
                if kout is None:
                    k.op(DVE, lambda: nc.vector.tensor_tensor(out=dst, in0=qn[:], in1=pss[:], op=ALU.mult), [bqn, bpss], dbufs)
                    return
                k.op(DVE, lambda: nc.vector.tensor_tensor(out=qn[:], in0=qn[:], in1=pss[:], op=ALU.mult), [bpss], [bqn])
                k.op(DVE, lambda: nc.vector.tensor_copy(out=dst, in_=qn[:]), [bqn], dbufs)
                yield
                kout(qn, bqn)

            def g_v(wl, s, t0):
                slot, bs = wget(wl)
                sv = slot[:, 0:KC * WKV].rearrange("p (c n) -> p c n", c=KC)
                ps, bps = bankA()
                mm_group(ps[:, 0:WV], [(hT[:, kc, s * 128:(s + 1) * 128], sv[:, kc, WV:2 * WV]) for kc in range(KC)], [bs] + b_hT, [bps])
                ch = t0 // 128 + s
                if S:
                    k.op(DVE, lambda: nc.vector.tensor_copy(out=VA[:, ch, :, 0:64], in_=ps[:, 0:WV].rearrange("p (h d) -> p h d", d=64)), [bps], [b_VA[ch]])
                else:
                    st_t, b_st = kvst[s % 2]
                    act_op(st_t[:, 0:WV], ps[:, 0:WV], AF.Copy, [bps], [b_st])
                    k.op(DVE, lambda: nc.vector.tensor_copy(out=VA[:, ch, :, 0:64], in_=st_t[:, 0:WV].rearrange("p (h d) -> p h d", d=64)), [b_st], [b_VA[ch]])
                    dst = (nva if even else nvs)[t0 + s * 128:t0 + (s + 1) * 128, :]
                    k.dma(SP, ds_kv[s % 2], [(dst, st_t[:, 0:WV], {})], [b_st], [])
                return
                yield

            def g_u(wl, g, t0):
                slot, bs = wget(wl)
                sv = slot[:].rearrange("p (c n) -> p c n", c=KC)
                ps, bps = bankA()
                mm_group(ps[:], [(sv[:, kc, g * 128:(g + 1) * 128], hT[:, kc, :]) for kc in range(KC)], [bs] + b_hT, [bps])
                act_op(QA[:, 4 + g, t0:t0 + 512], ps[:], AF.Gelu_apprx_tanh, [bps], qa_b([4 + g], t0, t0 + 512))
                return
                yield

            def g_vb(wl, s, t0):
                slot, bs = wget(wl)
                sv = slot[:].rearrange("p (c n) -> p c n", c=KC)
                (vn, bvn), (gv, bgv), (sst, bsst) = nxt()
                ps, bps = bankA()
                mm_group(ps[:], [(hT[:, kc, s * 128:(s + 1) * 128], sv[:, kc, :]) for kc in range(KC)], [bs] + b_hT, [bps])
                act_op(gv[:], ps[:], AF.Gelu_apprx_tanh, [bps], [bgv])
                act_op(vn[:], gv[:], AF.Square, [bgv], [bvn, bsst], accum_out=sst[:, 0:1])
                act_op(sst[:, 1:2], sst[:, 0:1], AF.Ln, [], [bsst], bias=EPS, scale=1.0 / 512)
                act_op(sst[:, 2:3], sst[:, 1:2], AF.Exp, [], [bsst], scale=-0.5)
                k.op(DVE, lambda: nc.vector.scalar_tensor_tensor(out=vn[:], in0=gv[:], scalar=sst[:, 2:3], in1=vgain[:], op0=ALU.mult, op1=ALU.mult), [bgv, bsst, b_const], [bvn])
                yield
                pm, bpm = bankA()
                for g in range(4):
                    mm_group(pm[:, g * 128:(g + 1) * 128], [(vn[:, g * 128:(g + 1) * 128], wsT[:, g, :])], [bvn, b_const], [bpm])
                k.op(DVE, lambda: nc.vector.tensor_tensor(out=pm[:], in0=pm[:], in1=bsbc[:], op=ALU.add), [b_const], [bpm])
                tt0 = t0 + s * 128
                qv = QA[:, 4:8, tt0:tt0 + 128]
                k.op(DVE, lambda: nc.vector.tensor_tensor(out=qv, in0=pm[:].rearrange("p (g n) -> p g n", g=4), in1=qv, op=ALU.mult), [bpm], qa_b(range(4, 8), tt0, tt0 + 128))

            def mk_kout(m, t0):
                def kout(qn, bqn):
                    for s in range(4):
                        st_t, b_st = kvst[s % 2]
                        pt, bpt = bankA()
                        transpose_to(pt[:, 0:128], qn[:, s * 128:(s + 1) * 128], [bqn], [bpt])
                        k.op(DVE, lambda: nc.vector.tensor_copy(out=st_t[:, 0:128], in_=pt[:, 0:128]), [bpt], [b_st])
                        dst = (nka if even else nks)[t0 + s * 128:t0 + (s + 1) * 128, m * 128:(m + 1) * 128]
                        k.dma(SP, ds_kv[s % 2], [(dst, st_t[:, 0:128], {})], [b_st], [])
                return kout

            for blk in range(nblk):
                t0 = blk * 512
                hT, b_hT = hTs[blk % len(hTs)], b_hTs[blk % len(hTs)]
                if blk == 0 or not dbl:
                    adaln(L, r, 0, t0, 512, hT, 0, b_hT, nxt)
                if blk == 0:
                    dump(f"hT_{grp}{L}", hT[:], [128, KC, 512], BF16)
                wls = wl_proj[blk]
                gens = []
                if dbl and blk + 1 < nblk:
                    gens.append(adaln_gen(L, r, 0, t0 + 512, 512, hTs[(blk + 1) % 2], 0, b_hTs[(blk + 1) % 2], anxt))
                for m in range(NQC // 4):
                    wls["q"][m].kind = "q"
                    for j in range(4):
                        c = 4 * m + j
                        gens.append(g_qk(wls["q"][m], j * 128, gq, QA[:, c, t0:t0 + 512], qa_b([c], t0, t0 + 512), t0, S, None))
                wls["kv"].kind = "kv"
                for m in range(NKC):
                    gens.append(g_qk(wls["kv"], m * 128, gk, KT[:, m, t0:t0 + 512], [b_KT[m][t0 // 128 + g] for g in range(4)], t0, S,
                                     None if S else mk_kout(m, t0)))
                for s in range(4):
                    gens.append(g_v(wls["kv"], s, t0))
                if even:
                    for g in range(4):
                        gens.append(g_u(wls["u"], g, t0))
                    for s_ in range(4):
                        gens.append(g_vb(wls["vb"], s_, t0))
                run_pipe(gens)

            if stop == f"proj_{grp}{L}":
                k.barrier()
                raise _Stop()
            dump(f"QA_{grp}{L}", QA[:], [128, 8, T], BF16)
            dump(f"KT_{grp}{L}", KT[:], [128, NKC, TK], BF16)
            dump(f"VA_{grp}{L}", VA[:], [128, TK // 128, NKV, 128], BF16)
            k.barrier()
            ar.top = mark2

            KTb = ar.alloc("KTb", [128, NKC, TK], BF16)
            b_KTb = Buf()
            allkt = [b for row in b_KT for b in row]
            k.op(DVE, lambda: nc.vector.memset(KTb[0:64, :, :], 0.0), [], [b_KTb])
            k.op(DVE, lambda: nc.vector.tensor_copy(out=KTb[64:128, :, :], in_=KT[64:128, :, :]), allkt, [b_KTb])
            k.op(DVE, lambda: nc.vector.memset(KT[64:128, :, :], 0.0), [], allkt)
            NPT = 3
            PT = [(ar.alloc("PT", [128, 2, 512], BF16), Buf()) for _ in range(NPT)]
            rl = [(ar.alloc("rl", [128, 512], F32), Buf()) for _ in range(1)]
            pt_i = [0]
            rl_i = [0]
            nb = negB[:, L:L + 1]
            units = []
            for c in range(NQC):
                for hf in range(2):
                    if S:
                        for rg in range(4):
                            chunks = [(T // 128 + t, 0, 512, False, False) for t in range(4)]
                            if even:
                                chunks += [(kb, 0, 512, False, False) for kb in range(16)]
                            else:
                                for kb in range(max(0, 4 * rg - 1), min(16, 4 * rg + 5)):
                                    lo = max(kb - 1, 4 * rg)
                                    hi = min(kb + 1, 4 * rg + 3)
                                    chunks.append((kb, (lo - 4 * rg) * 128, (hi - lo + 1) * 128, lo == kb - 1, hi == kb + 1))
                            units.append({"c": c, "hf": hf, "q0": rg * 512, "nq": 512, "chunks": chunks})
                    else:
                        for (s0, sl) in seqs:
                            units.append({"c": c, "hf": hf, "q0": s0, "nq": sl, "chunks": [(s0 // 128 + i, 0, sl, False, False) for i in range(sl // 128)]})
            ntiles = sum(len(u["chunks"]) for u in units)
            extras = [(lambda blk=blk: mod_block(0, blk, wl_mod0b[blk])) for blk in sorted(wl_mod0b)]
            extras += [(lambda blk=blk: mod_block(1, blk, wl_mod1[blk])) for blk in sorted(wl_mod1)]
            ex_every = max(1, ntiles // (len(extras) + 1)) if extras else 0

            def emit_qk_group(u, idxs):
                c, hf, q0 = u["c"], u["hf"], u["q0"]
                m = c // 4
                ktsrc = KT if hf == 0 else KTb
                _, c0, ncol, mlo, mhi = u["chunks"][idxs[0]]
                pt, bpt = PT[pt_i[0] % NPT]
                pt_i[0] += 1
                if len(idxs) == 2:
                    b0 = bankPair()
                    banks = [b0, b0 + 1]
                else:
                    _, bb = bankA()
                    banks = [b_ps.index(bb)]
                for j, i in enumerate(idxs):
                    kc_i = u["chunks"][i][0]
                    pso = psum[banks[j]]

                    def qk_fn():
                        inst = nc.tensor.matmul(pso[:, c0:c0 + ncol], lhsT=ktsrc[:, m, kc_i * 128:(kc_i + 1) * 128], rhs=QA[:, c, q0 + c0:q0 + c0 + ncol],
                                                start=True, stop=not (mlo or mhi))
                        if mlo:
                            inst = nc.tensor.matmul(pso[:, c0:c0 + 128], lhsT=ident_bf[:], rhs=maskB[:], start=False, stop=not mhi)
                        if mhi:
                            e0 = c0 + ncol - 128
                            inst = nc.tensor.matmul(pso[:, e0:e0 + 128], lhsT=ident_bf[:], rhs=maskA[:], start=False, stop=True)
                        return inst
                    k.op(PE, qk_fn, qa_b([c], q0, q0 + u["nq"]) + [b_KT[m][kc_i], b_KTb, b_const], [b_ps[banks[j]]])
                if len(idxs) == 2:
                    act_op(pt[:, :, c0:c0 + ncol], psall[:, banks[0]:banks[0] + 2, c0:c0 + ncol], AF.Exp, [b_ps[banks[0]], b_ps[banks[1]], b_const], [bpt], bias=nb, scale=0.125)
                else:
                    act_op(pt[:, 0, c0:c0 + ncol], psum[banks[0]][:, c0:c0 + ncol], AF.Exp, [b_ps[banks[0]], b_const], [bpt], bias=nb, scale=0.125)
                return [(u, i, pt[:, j, :], bpt) for j, i in enumerate(idxs)]

            def emit_pv(item):
                u, i, pt, bpt = item
                c, hf, q0, nq = u["c"], u["hf"], u["q0"], u["nq"]
                m = c // 4
                kv = 2 * m + hf
                head = 8 * m + 4 * hf + (c % 4)
                p0 = 64 * hf
                kc_i, c0, ncol, _, _ = u["chunks"][i]
                if i == 0:
                    u["acc"] = bankB()
                acc, bacc = u["acc"]
                last = i == len(u["chunks"]) - 1
                k.op(PE, lambda: nc.tensor.matmul(acc[:, c0:c0 + ncol], lhsT=VA[:, kc_i, kv, :], rhs=pt[:, c0:c0 + ncol], start=(i == 0), stop=last),
                     [bpt, b_VA[kc_i]], [bacc])
                if not last:
                    return
                norm_q.append((tcount_ref[0] + NORM_DELAY, lambda: emit_norm(u)))

            def emit_norm(u):
                c, hf, q0, nq = u["c"], u["hf"], u["q0"], u["nq"]
                m = c // 4
                head = 8 * m + 4 * hf + (c % 4)
                p0 = 64 * hf
                acc, bacc = u["acc"]
                rt, brt = rl[0]
                rl_i[0] += 1
                if not even and S and (u["q0"] // 512 + u["hf"]) % 2 == 0:
                    k.op(DVE, lambda: nc.vector.tensor_scalar(out=rt[64:128, 0:nq], in0=acc[64:128, 0:nq], scalar1=esink[64:128, head:head + 1], scalar2=None, op0=ALU.add), [bacc, b_const], [brt])
                    k.op(DVE, lambda: nc.vector.reciprocal(out=rt[64:128, 0:nq], in_=rt[64:128, 0:nq]), [], [brt])
                elif not even:
                    act_op(rt[64:128, 0:nq], acc[64:128, 0:nq], AF.Ln, [bacc, b_const], [brt], bias=esink[64:128, head:head + 1])
                    act_op(rt[64:128, 0:nq], rt[64:128, 0:nq], AF.Exp, [], [brt], scale=-1.0)
                elif not S:
                    act_op(rt[64:128, 0:nq], acc[64:128, 0:nq], AF.Ln, [bacc], [brt])
                    act_op(rt[64:128, 0:nq], rt[64:128, 0:nq], AF.Exp, [], [brt], scale=-1.0)
                else:
                    k.op(DVE, lambda: nc.vector.reciprocal(out=rt[64:128, 0:nq], in_=acc[64:128, 0:nq]), [bacc], [brt])
                k.op(DVE, lambda: nc.vector.tensor_tensor(out=QA[p0:p0 + 64, c, q0:q0 + nq], in0=acc[0:64, 0:nq], in1=rt[64:128, 0:nq], op=ALU.mult), [bacc, brt], qa_b([c], q0, q0 + nq))

            LAG = 3
            pend = []
            tcount = 0
            tcount_ref = [0]
            norm_q = []
            NORM_DELAY = 4 if S else 2
            for u in units:
                ch = u["chunks"]
                i = 0
                while i < len(ch):
                    if i + 1 < len(ch) and ch[i][1:3] == ch[i + 1][1:3] and not (ch[i][3] or ch[i][4] or ch[i + 1][3] or ch[i + 1][4]):
                        idxs = [i, i + 1]
                    else:
                        idxs = [i]
                    pend.append(emit_qk_group(u, idxs))
                    i += len(idxs)
                    while len(pend) > 2:
                        for it in pend.pop(0):
                            emit_pv(it)
                    for _ in idxs:
                        tcount += 1
                        tcount_ref[0] = tcount
                        if extras and tcount % ex_every == 0:
                            extras.pop(0)()
                    while norm_q and norm_q[0][0] <= tcount:
                        norm_q.pop(0)[1]()
            while pend:
                for it in pend.pop(0):
                    emit_pv(it)
            while norm_q:
                norm_q.pop(0)[1]()
            while extras:
                extras.pop(0)()
            if wl_mod1:
                mod_finish(0, (32,))
                mod_finish(1)

            if stop == f"att_{grp}{L}":
                k.barrier()
                raise _Stop()
            dump(f"AT_{grp}{L}", QA[:], [128, 8, T], BF16)
            k.barrier()
            ar.top = mark2

            if S:
                sblocks = [[(0, 1024, "zero", "halo")], [(1024, 2048, "saved", "zero")]]
            else:
                sblocks = [[(i * 256, (i + 1) * 256, "zero", "zero") for i in range(4)]]
            ar.top = mark_qa
            hi_mark = ar.hi
            ncols_max = max(sum(b - a + 2 for (a, b, _, _) in segs) for segs in sblocks)
            h2 = ar.alloc_hi("h2", [128, KC, ncols_max], BF16)
            b_h2 = [Buf() for _ in range(KC)]
            fsets = []
            for _ in range(2):
                fsets.append(((ar.alloc_hi("fB", [128, 512], BF16), Buf()), (ar.alloc_hi("fF", [128, 512], F32), Buf()), (None, None)))
            fs_i = [0]

            def fnxt():
                i = fs_i[0] % 2
                fs_i[0] += 1
                return fsets[i]

            def ffn_plan(segs):
                pieces = []
                windows = []
                off = 0
                for (a, b, lh, rh) in segs:
                    pre = []
                    if lh == "zero":
                        pre.append(lambda off=off: k.op(DVE, lambda: nc.vector.memset(h2[:, :, off:off + 1], 0.0), [], b_h2))
                    elif lh == "saved":
                        pre.append(lambda off=off: k.op(DVE, lambda: nc.vector.tensor_copy(out=h2[:, :, off:off + 1], in_=halo[:, :, 0:1]), [b_halo], b_h2))
                    if rh == "zero":
                        pre.append(lambda off=off, a=a, b=b: k.op(DVE, lambda: nc.vector.memset(h2[:, :, off + 1 + b - a:off + 2 + b - a], 0.0), [], b_h2))
                    tb = b + (1 if rh == "halo" else 0)
                    t = a
                    first = True
                    while t < tb:
                        n = min(512, tb - t)
                        last = t + n >= tb

                        def mk(t=t, n=n, off=off, a=a, b=b, pre=(pre if first else []), post_halo=(last and rh == "halo")):
                            for f in pre:
                                f()
                            yield from adaln_gen(L, r, 1, t, n, h2, off + 1 + (t - a), b_h2, fnxt)
                            if post_halo:
                                k.op(DVE, lambda: nc.vector.tensor_copy(out=halo[:, :, 0:1], in_=h2[:, :, off + b - a:off + b - a + 1]), b_h2, [b_halo])
                        pieces.append((t + n, mk))
                        first = False
                        t += n
                    nw = (b - a + 509) // 510
                    base, rem = divmod(b - a, nw)
                    o = a
                    for wi in range(nw):
                        ln = base + (1 if wi < rem else 0)
                        windows.append((off + (o - a), ln + 2, o, ln))
                        o += ln
                    off += b - a + 2
                return pieces, windows

            plans = [ffn_plan(segs) for segs in sblocks]
            nstep = 2 if S else 3

            def step_gens(active, n):
                for _ in range(n):
                    if not active:
                        return
                    g = active[0]
                    try:
                        next(g)
                    except StopIteration:
                        active.pop(0)

            def drain(active):
                while active:
                    step_gens(active, 1)

            slots = [wget(wl_out[cb]) for cb in range(2)]
            pieces0 = list(plans[0][0])
            active = []
            for blk in range(nblk):
                t0 = blk * 512
                for co in range(KC):
                    slot, bs = slots[co // 4]
                    sv = slot[:].rearrange("p (c n) -> p c n", c=KC)
                    oc = co % 4
                    ps, bps = bankA()
                    mm_group(ps[:], [(sv[:, kc, oc * 128:(oc + 1) * 128], QA[:, kc, t0:t0 + 512]) for kc in range(KC)], [bs] + qa_b(range(8), t0, t0 + 512), [bps])
                    k.op(DVE, lambda: nc.vector.scalar_tensor_tensor(
                        out=xT[:, co, t0:t0 + 512], in0=ps[:], scalar=modT[:, L, r, 16 + co:17 + co], in1=xT[:, co, t0:t0 + 512],
                        op0=ALU.mult, op1=ALU.add), [bps, bm], xb([co], t0, t0 + 512))
                    step_gens(active, nstep)
                while pieces0 and pieces0[0][0] <= t0 + 512:
                    active.append(pieces0.pop(0)[1]())
            while pieces0:
                active.append(pieces0.pop(0)[1]())
            drain(active)

            if stop == f"wout_{grp}{L}":
                k.barrier()
                raise _Stop()
            dump(f"XM_{grp}{L}", xT[:, :, 0:T], [128, KC, T], F32)
            k.barrier()
            ar.top = mark

            actT = ar.alloc("actT", [128, FC, 1024], BF16)
            NZ = 7
            zt = [(ar.alloc("zt", [128, 512], F32), Buf()) for _ in range(NZ)]
            zi = [0]

            def ztmp():
                i = zi[0] % NZ
                zi[0] += 1
                return zt[i]

            b_act = [[Buf() for _ in plans[0][1]] for _ in range(FC)]
            for sbi, segs in enumerate(sblocks):
                windows = plans[sbi][1]
                assert len(windows) == len(plans[0][1])
                sb0 = segs[0][0]
                pend_fin = []
                for u in range(FC // 2):
                    slot, bs = wget(wl_ffn[sbi]["u"][u])
                    sv = slot[:].rearrange("p (c n) -> p c n", c=KC)
                    j0 = 2 * u
                    for jj in range(2):
                        j = j0 + jj
                        for wi, (c0, N, o, ln) in enumerate(windows):
                            res = []
                            parts = []
                            for part in range(2):
                                col = part * 256 + jj * 128
                                jc = j + part * FC
                                ps, bps = bankA()
                                mm_group(ps[:, 0:N], [(sv[:, kc, col:col + 128], h2[:, kc, c0:c0 + N]) for kc in range(KC)], [bs] + b_h2, [bps])
                                tz, btz = ztmp()
                                act_op(tz[:, 0:ln], ps[:, 1:1 + ln], AF.Identity, [bps, b_const], [btz], bias=cwT[:, L, 3, jc:jc + 1], scale=cwT[:, L, 1, jc:jc + 1])
                                parts.append((ps, bps, tz, btz, jc))
                                res.append((tz, btz))
                            for tap, lo in ((0, 0), (2, 2)):
                                for (ps, bps, tz, btz, jc) in parts:
                                    k.op(DVE, lambda: nc.vector.scalar_tensor_tensor(out=tz[:, 0:ln], in0=ps[:, lo:lo + ln], scalar=cwT[:, L, tap, jc:jc + 1], in1=tz[:, 0:ln], op0=ALU.mult, op1=ALU.add), [bps, b_const], [btz])
                            (ta_, bta), (tg_, btg) = res
                            sg, bsg = ztmp()

                            def fin(ta_=ta_, bta=bta, tg_=tg_, btg=btg, sg=sg, bsg=bsg, j=j, o=o, ln=ln, wi=wi):
                                act_op(sg[:, 0:ln], tg_[:, 0:ln], AF.Silu, [btg], [bsg])
                                k.op(POOL, lambda: nc.gpsimd.tensor_tensor(out=actT[:, j, o - sb0:o - sb0 + ln], in0=ta_[:, 0:ln], in1=sg[:, 0:ln], op=ALU.mult), [bta, bsg], [b_act[j][wi]])
                            if pend_fin:
                                pend_fin.pop(0)()
                            pend_fin.append(fin)
                while pend_fin:
                    pend_fin.pop(0)()
                active = [mk() for (_, mk) in plans[sbi + 1][0]] if sbi + 1 < len(sblocks) else []
                for co in range(KC):
                    slot, bs = wget(wl_ffn[sbi]["d"][co])
                    sv = slot[:, 0:FC * 128].rearrange("p (f n) -> p f n", f=FC)
                    for wi, (c0, N, o, ln) in enumerate(windows):
                        ps, bps = bankA()
                        mm_group(ps[:, 0:ln], [(sv[:, f, :], actT[:, f, o - sb0:o - sb0 + ln]) for f in range(FC)], [bs] + [b_act[f][wi] for f in range(FC)], [bps])
                        k.op(DVE, lambda: nc.vector.scalar_tensor_tensor(
                            out=xT[:, co, o:o + ln], in0=ps[:, 0:ln], scalar=modT[:, L, r, 40 + co:41 + co], in1=xT[:, co, o:o + ln],
                            op0=ALU.mult, op1=ALU.add), [bps, bm], xb([co], o, o + ln))
                        step_gens(active, 2)
                drain(active)
            k.barrier()
            ar.top = mark
            ar.hi = hi_mark
            if stop == f"ffn_{grp}{L}":
                raise _Stop()
            dump(f"XF_{grp}{L}", xT[:, :, 0:T], [128, KC, T], F32)

        mark = ar.top
        x_stage = [(ar.alloc("xst", [128, 1024], F32), Buf()) for _ in range(2)]
        store_x(ys if S else yp, T)
        k.barrier()
        ar.top = mark

    try:
        run_group("S")
        run_group("P")
    except _Stop:
        pass
    k.final()
    return nc


_NC_CACHE = {}


def kernel(x_prompt, x_sample, cache_k_attn, cache_v_attn, cache_k_swa, cache_v_swa, c, c_ctx,
           w_mod, b_mod, w_in_even, q_norm_a, k_norm_a, v_norm_b, w_spatial, b_spatial, w_out_even,
           w_in_odd, q_norm_c, k_norm_c, sink_c, w_out_odd, w_up, conv_w, conv_b, w_down):
    f = lambda a: np.ascontiguousarray(np.asarray(a, dtype=np.float32))
    if "nc" not in _NC_CACHE:
        _NC_CACHE["nc"] = build()
    nc = _NC_CACHE["nc"]
    shared = {
        "w_mod": f(w_mod), "b_mod": f(b_mod), "w_in_even": f(w_in_even[0]),
        "q_norm_a": f(q_norm_a), "k_norm_a": f(k_norm_a), "v_norm_b": f(v_norm_b),
        "w_spatial": f(w_spatial[0]), "b_spatial": f(b_spatial).reshape(1, 512),
        "w_out_even": f(w_out_even[0]), "w_in_odd": f(w_in_odd[0]),
        "q_norm_c": f(q_norm_c), "k_norm_c": f(k_norm_c), "sink_c": f(sink_c),
        "w_out_odd": f(w_out_odd[0]), "w_up": f(w_up), "conv_w": f(conv_w), "conv_b": f(conv_b),
        "w_down": f(w_down),
    }
    x_prompt = f(x_prompt); x_sample = f(x_sample); c = f(c); c_ctx = f(c_ctx)
    in_maps = []
    for i in range(N_CORES):
        m = dict(shared)
        m["xs"] = x_sample[i]
        m["xp"] = x_prompt[4 * i:4 * i + 4].reshape(TP, D)
        m["cka"] = f(cache_k_attn[i, 0]).reshape(512, 128)
        m["cva"] = f(cache_v_attn[i, 0]).reshape(512, 128)
        m["cks"] = f(cache_k_swa[i, 0]).reshape(512, 256)
        m["cvs"] = f(cache_v_swa[i, 0]).reshape(512, 256)
        m["cvec"] = np.ascontiguousarray(np.stack([c[i], c_ctx], axis=0))
        in_maps.append(m)
    res = run_bass_kernel_spmd(nc, in_maps, core_ids=list(range(N_CORES)))
    R = res.results
    y_prompt = np.concatenate([R[i]["yp"].reshape(4, 256, D) for i in range(N_CORES)], axis=0)
    y_sample = np.stack([R[i]["ys"] for i in range(N_CORES)], axis=0)
    nka = np.concatenate([R[i]["nka"].reshape(4, 1, 256, 2, 64) for i in range(N_CORES)], axis=0)
    nva = np.concatenate([R[i]["nva"].reshape(4, 1, 256, 2, 64) for i in range(N_CORES)], axis=0)
    nks = np.concatenate([R[i]["nks"].reshape(4, 1, 256, 4, 64) for i in range(N_CORES)], axis=0)
    nvs = np.concatenate([R[i]["nvs"].reshape(4, 1, 256, 4, 64) for i in range(N_CORES)], axis=0)
    return (y_prompt.astype(np.float32), y_sample.astype(np.float32), nka.astype(np.float32),
            nva.astype(np.float32), nks.astype(np.float32), nvs.astype(np.float32))
```

```python
import os
import numpy as np
import concourse.bass as bass
import concourse.mybir as mybir
from concourse.bass_utils import run_bass_kernel_spmd

F32 = mybir.dt.float32
BF16 = mybir.dt.bfloat16
I32 = mybir.dt.int32
U8 = mybir.dt.uint8
AF = mybir.ActivationFunctionType
ALU = mybir.AluOpType

N_CORES = 8
D = 1024
KC = 8
TS = 2048
TP = 1024
FF = 2816
FC = 22
EPS = 1e-6
LN_THETA = float(np.log(10000.0))


class Buf:
    __slots__ = ("w", "r", "t")

    def __init__(self):
        self.w = None
        self.r = {}
        self.t = 0


class Eng:
    def __init__(self, nc, h, name):
        self.h = h
        self.name = name
        self.sem = nc.alloc_semaphore("es_" + name)
        self.count = 0
        self.known = {}


class DSem:
    def __init__(self, nc, name):
        self.name = name
        self.sem = nc.alloc_semaphore("ds_" + name)
        self.count = 0


class K:
    def __init__(self, nc):
        self.nc = nc
        self.pe = Eng(nc, nc.tensor, "pe")
        self.act = Eng(nc, nc.scalar, "act")
        self.dve = Eng(nc, nc.vector, "dve")
        self.pool = Eng(nc, nc.gpsimd, "pool")
        self.sp = Eng(nc, nc.sync, "sp")
        self.engs = [self.pe, self.act, self.dve, self.pool, self.sp]
        self.sems = {e.name: e for e in self.engs}
        self.dsems = {}
        self.nops = 0
        self.clock = 0
        self.after_barrier = None
        self.snap = {}

    def dsem(self, name):
        d = DSem(self.nc, name)
        self.dsems[name] = d
        self.sems[name] = d
        return d

    def _waits(self, eng, reads, writes):
        deps = {}
        for b in reads:
            if b.w is not None:
                k, v = b.w
                if deps.get(k, 0) < v:
                    deps[k] = v
        for b in writes:
            if b.w is not None:
                k, v = b.w
                if deps.get(k, 0) < v:
                    deps[k] = v
            for k, v in b.r.items():
                if deps.get(k, 0) < v:
                    deps[k] = v
        for k, v in sorted(deps.items(), key=lambda kv: -kv[1]):
            if eng is self.pe and k == eng.name:
                continue
            if eng.known.get(k, 0) < v:
                eng.h.wait_ge(self.sems[k].sem, v)
                eng.known[k] = v
                for kk, vv in self.snap.get((k, v), ()):
                    if eng.known.get(kk, 0) < vv:
                        eng.known[kk] = vv

    def _commit(self, ev, reads, writes):
        k, v = ev
        self.clock += 1
        for b in writes:
            b.w = ev
            b.r = {}
            b.t = self.clock
        for b in reads:
            if b.r.get(k, 0) < v:
                b.r[k] = v
            b.t = self.clock

    def op(self, eng, fn, reads=(), writes=()):
        self._waits(eng, reads, writes)
        inst = fn()
        eng.count += 1
        inst.then_inc(eng.sem, 1)
        self.nops += 1
        self.snap[(eng.name, eng.count)] = tuple(eng.known.items())
        self._commit((eng.name, eng.count), [b for b in reads if b not in writes], writes)

    def dma(self, q, ds, pieces, reads=(), writes=()):
        self._waits(q, reads, writes)
        if ds.count and q.known.get(ds.name, 0) < ds.count:
            q.h.wait_ge(ds.sem, ds.count)
            q.known[ds.name] = ds.count
        for (o, i, kw) in pieces:
            q.h.dma_start(out=o, in_=i, **kw).then_inc(ds.sem, 16)
            ds.count += 16
        self.snap[(ds.name, ds.count)] = tuple(q.known.items())
        self._commit((ds.name, ds.count), [b for b in reads if b not in writes], writes)

    def barrier(self, engs=None):
        for e in (engs or [self.pe, self.act, self.dve, self.sp]):
            for name, s in self.sems.items():
                if s.count and e.known.get(name, 0) < s.count and name != e.name:
                    e.h.wait_ge(s.sem, s.count)
                    e.known[name] = s.count
            if e.count:
                e.h.wait_ge(e.sem, e.count)
        if self.after_barrier is not None:
            self.after_barrier()

    def final(self):
        e = self.sp
        for name, s in self.sems.items():
            if s.count and name != e.name:
                e.h.wait_ge(s.sem, s.count)


class Arena:
    def __init__(self, nc, nbytes):
        self.nc = nc
        nc.alloc_sbuf_tensor("arena", [128, nbytes], U8)
        self.base = None
        for a in list(nc.allocations):
            if getattr(a, "name", "") == "arena_set":
                self.base = a.memorylocations[0].addr
        assert self.base is not None
        self.size = nbytes
        self.top = 0
        self.hi = nbytes
        self.n = 0

    def alloc_hi(self, name, shape, dtype):
        esz = {F32: 4, BF16: 2, I32: 4}[dtype]
        nb = esz * int(np.prod(shape[1:]))
        nb = (nb + 31) // 32 * 32
        self.hi -= nb
        assert self.hi >= self.top, f"SBUF arena overflow (hi) at {name}"
        t = self.nc.alloc_sbuf_tensor_at(f"{name}_{self.n}", list(shape), dtype, offset=self.base + self.hi)
        self.n += 1
        return t

    def alloc(self, name, shape, dtype):
        esz = {F32: 4, BF16: 2, I32: 4}[dtype]
        nb = esz * int(np.prod(shape[1:]))
        nb = (nb + 31) // 32 * 32
        assert self.top + nb <= self.hi, f"SBUF arena overflow at {name}: {self.top}+{nb}>{self.hi}"
        t = self.nc.alloc_sbuf_tensor_at(f"{name}_{self.n}", list(shape), dtype, offset=self.base + self.top)
        self.n += 1
        self.top += nb
        return t


class _Stop(Exception):
    pass


def build(dbg=False, stop=None):
    nc = bass.Bass("TRN2", target_bir_lowering=False)
    k = K(nc)
    PE, ACT, DVE, POOL, SP = k.pe, k.act, k.dve, k.pool, k.sp

    def din(name, shape):
        return nc.dram_tensor(name, list(shape), F32, kind="ExternalInput").ap()

    def dout(name, shape):
        return nc.dram_tensor(name, list(shape), F32, kind="ExternalOutput").ap()

    xs = din("xs", [TS, D])
    xp = din("xp", [TP, D])
    cka = din("cka", [512, 128])
    cva = din("cva", [512, 128])
    cks = din("cks", [512, 256])
    cvs = din("cvs", [512, 256])
    cvec = din("cvec", [2, D])
    w_mod = din("w_mod", [2, D, 6 * D])
    b_mod = din("b_mod", [2, 6 * D])
    w_in_even = din("w_in_even", [D, 1792])
    q_norm_a = din("q_norm_a", [1, 64])
    k_norm_a = din("k_norm_a", [1, 64])
    v_norm_b = din("v_norm_b", [1, 512])
    w_spatial = din("w_spatial", [4, 128, 128])
    b_spatial = din("b_spatial", [1, 512])
    w_out_even = din("w_out_even", [D, D])
    w_in_odd = din("w_in_odd", [D, 1536])
    q_norm_c = din("q_norm_c", [1, 64])
    k_norm_c = din("k_norm_c", [1, 64])
    sink_c = din("sink_c", [1, 16])
    w_out_odd = din("w_out_odd", [D, D])
    w_up = din("w_up", [2, D, 2 * FF])
    conv_w = din("conv_w", [2, 3, 2 * FF])
    conv_b = din("conv_b", [2, 2 * FF])
    w_down = din("w_down", [2, FF, D])

    ys = dout("ys", [TS, D])
    yp = dout("yp", [TP, D])
    nka = dout("nka", [TP, 128])
    nva = dout("nva", [TP, 128])
    nks = dout("nks", [TP, 256])
    nvs = dout("nvs", [TP, 256])

    ar = Arena(nc, 208000)
    xT = ar.alloc("xT", [128, KC, TS], F32)
    ropeC = ar.alloc("ropeC", [128, TS], F32)
    ropeS = ar.alloc("ropeS", [128, TS], F32)
    NSLOT = 4
    ring = [ar.alloc(f"ring{i}", [128, 4096], BF16) for i in range(NSLOT)]
    ident = ar.alloc("ident", [128, 128], F32)
    rmat = ar.alloc("rmat", [128, 128], F32)
    bones = ar.alloc("bones", [128, 128], BF16)
    ones = ar.alloc("ones", [128, 128], BF16)
    maskA = ar.alloc("maskA", [128, 128], BF16)
    maskB = ar.alloc("maskB", [128, 128], BF16)
    ident_bf = ar.alloc("ident_bf", [128, 128], BF16)
    modT = ar.alloc("modT", [128, 2, 2, 48], F32)
    cwT = ar.alloc("cwT", [128, 2, 4, 44], F32)
    gains = ar.alloc("gains", [128, 4], F32)
    negB = ar.alloc("negB", [128, 2], F32)
    esink = ar.alloc("esink", [128, 16], F32)
    vgain = ar.alloc("vgain", [128, 512], F32)
    bsbc = ar.alloc("bsbc", [128, 512], F32)
    wsT = ar.alloc("wsT", [128, 4, 128], BF16)
    scT = ar.alloc("scT", [128, KC, 2], BF16)
    bmT = ar.alloc("bmT", [128, 2, 48], F32)
    halo = ar.alloc("halo", [128, KC, 2], BF16)
    b_halo = Buf()
    ptok = ar.alloc("ptok", [128, 8], F32)
    b_phase = Buf()
    k.after_barrier = lambda: k.op(DVE, lambda: nc.vector.memset(ptok[:, 0:1], 0.0), [], [b_phase])
    UNION = ar.top

    b_xT = [[Buf() for _ in range(TS // 128)] for _ in range(KC)]
    b_ring = [Buf() for _ in range(NSLOT)]
    ds_ring = [k.dsem(f"ring{i}") for i in range(NSLOT)]
    b_const = Buf()
    b_modL = [Buf(), Buf()]
    ds_setup = k.dsem("setup")
    ds_ld = [k.dsem("ld0"), k.dsem("ld1")]
    ds_st = [k.dsem("st0"), k.dsem("st1")]
    ds_kv = [k.dsem("kv0"), k.dsem("kv1")]
    ds_cache = k.dsem("cache")

    def xb(chunks, t0, t1):
        return [b_xT[c][g] for c in chunks for g in range(t0 // 128, (t1 + 127) // 128)]

    psall = nc.alloc_psum_tensor("psall", [128, 8, 512], F32)
    psum = [psall[:, i, :] for i in range(8)]
    b_ps = [Buf() for _ in range(8)]
    pa_state = [0]
    pb_state = [0]

    def bankA():
        i = min(range(0, 5), key=lambda j: b_ps[j].t)
        k.clock += 1
        b_ps[i].t = k.clock
        return psum[i], b_ps[i]

    def bankPair():
        b0 = min((0, 2), key=lambda j: max(b_ps[j].t, b_ps[j + 1].t))
        k.clock += 1
        b_ps[b0].t = k.clock
        b_ps[b0 + 1].t = k.clock
        return b0

    def bankB():
        i = min(range(5, 8), key=lambda j: b_ps[j].t)
        k.clock += 1
        b_ps[i].t = k.clock
        return psum[i], b_ps[i]

    wq = []
    wq_pos = [0]

    class WL:
        def __init__(self, fn):
            self.fn = fn
            self.slot = None
            self.bs = None
            self.idx = len(wq)
            wq.append(self)

    def wget(wl, ahead=2):
        tgt = min(len(wq), wl.idx + 1 + ahead)
        while wq_pos[0] < tgt:
            w = wq[wq_pos[0]]
            slot, bs, dss = ring_next()
            w.fn(slot, bs, dss)
            w.slot, w.bs = slot, bs
            wq_pos[0] += 1
        return wl.slot, wl.bs

    def run_pipe(gens):
        active = []
        it = iter(gens)
        more = True
        while more or active:
            g = next(it, None) if more else None
            if g is None:
                more = False
            else:
                try:
                    next(g)
                    active.append(g)
                except StopIteration:
                    pass
            for a in list(active):
                if a is g:
                    continue
                try:
                    next(a)
                except StopIteration:
                    active.remove(a)

    ring_state = [0]

    def ring_next():
        i = ring_state[0] % NSLOT
        ring_state[0] += 1
        return ring[i], b_ring[i], ds_ring[i]

    ds_dbg = k.dsem("dbg")
    dbg_names = set(dbg) if dbg else set()

    def dump(name, t, shape, dtype):
        if name not in dbg_names:
            return
        dt_ = nc.dram_tensor("dbg_" + name, list(shape), dtype, kind="ExternalOutput").ap()
        k.barrier()
        k.dma(SP, ds_dbg, [(dt_, t, {})], [], [])
        SP.h.wait_ge(ds_dbg.sem, ds_dbg.count)
        SP.known[ds_dbg.name] = ds_dbg.count
        k.barrier()

    def act_op(out, in_, func, reads, writes, bias=None, scale=None, accum_out=None):
        kw = {}
        if bias is not None:
            kw["bias"] = bias
        if scale is not None:
            kw["scale"] = scale
        if accum_out is not None:
            kw["accum_out"] = accum_out
        k.op(ACT, lambda: nc.scalar.activation(out=out, in_=in_, func=func, **kw), reads, writes)

    def mm_group(out, pairs, reads, writes):
        def fn():
            inst = None
            n = len(pairs)
            for i, (l, r) in enumerate(pairs):
                inst = nc.tensor.matmul(out, lhsT=l, rhs=r, start=(i == 0), stop=(i == n - 1))
            return inst
        k.op(PE, fn, reads, writes)

    def transpose_to(out_ps, in_sb, reads, writes, kp=128):
        k.op(PE, lambda: nc.tensor.transpose(out_ps, in_sb, ident[0:kp, 0:kp]), reads + [b_const], writes)

    x_stage = None
    def load_x(src, T):
        for _ in load_x_gen(src, T):
            pass

    def load_x_gen(src, T):
        for t in range(T // 128):
            sl = t % 2
            st_t, b_st = x_stage[sl]
            k.dma(SP, ds_ld[sl], [(st_t[:], src[t * 128:(t + 1) * 128, :], {})], [], [b_st])
            for half in range(2):
                pt, bpt = bankA()
                for cc in range(4):
                    c = half * 4 + cc
                    transpose_to(pt[:, cc * 128:(cc + 1) * 128], st_t[:, c * 128:(c + 1) * 128], [b_st], [bpt])
                dstv = xT[:, half * 4:half * 4 + 4, t * 128:(t + 1) * 128]
                srcv = pt[:].rearrange("p (c n) -> p c n", c=4)
                wb = [b_xT[c][t] for c in range(half * 4, half * 4 + 4)]
                if half == 0:
                    act_op(dstv, srcv, AF.Copy, [bpt], wb)
                else:
                    k.op(DVE, lambda: nc.vector.tensor_copy(out=dstv, in_=srcv), [bpt], wb)
            yield

    def store_x(dst, T):
        for t in range(T // 128):
            sl = t % 2
            st_t, b_st = x_stage[sl]
            for half in range(2):
                pt, bpt = bankA()
                for cc in range(4):
                    c = half * 4 + cc
                    transpose_to(pt[:, cc * 128:(cc + 1) * 128], xT[:, c, t * 128:(t + 1) * 128], [b_xT[c][t]], [bpt])
                if half == 0:
                    act_op(st_t[:, 0:512], pt[:], AF.Copy, [bpt], [b_st])
                else:
                    k.op(DVE, lambda: nc.vector.tensor_copy(out=st_t[:, 512:1024], in_=pt[:]), [bpt], [b_st])
            k.dma(SP, ds_st[sl], [(dst[t * 128:(t + 1) * 128, :], st_t[:], {})], [b_st], [])

    tmp_i = ar.alloc("tmp_i", [128, 128], I32)
    tmp_f = ar.alloc("tmp_f", [128, 128], F32)
    tmp_g = ar.alloc("tmp_g", [128, 2048], F32)
    tmp_h = ar.alloc("tmp_h", [128, 2048], F32)
    tmp_s = ar.alloc("tmp_s", [128, 8], F32)
    stg = ar.alloc("stg", [128, 1024], F32)
    b_ti, b_tf, b_tg, b_th, b_ts, b_stg = Buf(), Buf(), Buf(), Buf(), Buf(), Buf()

    k.op(POOL, lambda: nc.gpsimd.iota(tmp_i[:], pattern=[[1, 128]], base=0, channel_multiplier=-1), [], [b_ti])
    k.op(DVE, lambda: nc.vector.tensor_copy(out=tmp_f[:], in_=tmp_i[:]), [b_ti], [b_tf])
    k.op(DVE, lambda: nc.vector.tensor_single_scalar(out=ident[:], in_=tmp_f[:], scalar=0.0, op=ALU.is_equal), [b_tf], [b_const])
    k.op(DVE, lambda: nc.vector.tensor_scalar(out=maskA[:], in0=tmp_f[:], scalar1=0.0, scalar2=-30000.0, op0=ALU.is_gt, op1=ALU.mult), [b_tf], [b_const])
    k.op(DVE, lambda: nc.vector.tensor_scalar(out=maskB[:], in0=tmp_f[:], scalar1=0.0, scalar2=-30000.0, op0=ALU.is_lt, op1=ALU.mult), [b_tf], [b_const])
    k.op(DVE, lambda: nc.vector.tensor_single_scalar(out=ident_bf[:], in_=tmp_f[:], scalar=0.0, op=ALU.is_equal), [b_tf], [b_const])
    k.op(DVE, lambda: nc.vector.memset(ones[:], 1.0), [], [b_const])
    k.op(DVE, lambda: nc.vector.memset(bones[:], 0.0), [], [b_const])
    k.op(DVE, lambda: nc.vector.memset(bones[0:64, 0:64], 1.0), [], [b_const])
    k.op(DVE, lambda: nc.vector.memset(bones[64:128, 64:128], 1.0), [], [b_const])
    tmp_pi = ar.alloc("tmp_pi", [128, 4], I32)
    b_pi = Buf()
    k.op(POOL, lambda: nc.gpsimd.iota(tmp_pi[:, 0:1], pattern=[[0, 1]], base=0, channel_multiplier=1), [], [b_pi])
    k.op(DVE, lambda: nc.vector.tensor_single_scalar(out=tmp_pi[:, 1:2], in_=tmp_pi[:, 0:1], scalar=1, op=ALU.bitwise_and), [b_pi], [b_pi])
    k.op(DVE, lambda: nc.vector.tensor_scalar(out=tmp_pi[:, 2:3], in0=tmp_pi[:, 0:1], scalar1=5, scalar2=1, op0=ALU.arith_shift_right, op1=ALU.bitwise_and), [b_pi], [b_pi])
    k.op(DVE, lambda: nc.vector.tensor_scalar(out=tmp_pi[:, 3:4], in0=tmp_pi[:, 0:1], scalar1=1, scalar2=15, op0=ALU.arith_shift_right, op1=ALU.bitwise_and), [b_pi], [b_pi])
    k.op(DVE, lambda: nc.vector.tensor_copy(out=tmp_s[:, 0:4], in_=tmp_pi[:, 0:4]), [b_pi], [b_ts])
    act_op(tmp_s[:, 4:5], tmp_s[:, 3:4], AF.Exp, [b_ts], [b_ts], scale=-LN_THETA / 16.0)
    k.op(DVE, lambda: nc.vector.tensor_scalar(out=tmp_s[:, 5:6], in0=tmp_s[:, 1:2], scalar1=-1.0, scalar2=1.0, op0=ALU.mult, op1=ALU.add), [b_ts], [b_ts])
    tmp_f2 = ar.alloc("tmp_f2", [128, 128], F32)
    b_tf2 = Buf()
    k.op(DVE, lambda: nc.vector.tensor_scalar(out=rmat[:], in0=tmp_f[:], scalar1=1.0, scalar2=tmp_s[:, 5:6], op0=ALU.is_equal, op1=ALU.mult), [b_tf, b_ts], [b_const])
    k.op(DVE, lambda: nc.vector.tensor_scalar(out=tmp_f2[:], in0=tmp_f[:], scalar1=-1.0, scalar2=tmp_s[:, 1:2], op0=ALU.is_equal, op1=ALU.mult), [b_tf, b_ts], [b_tf2])
    k.op(DVE, lambda: nc.vector.tensor_tensor(out=rmat[:], in0=rmat[:], in1=tmp_f2[:], op=ALU.subtract), [b_tf2], [b_const])

    b_rope = Buf()
    b_par = Buf()
    def setup_dma(out, in_, q=SP, **kw):
        k.dma(q, ds_setup, [(out, in_, kw)], [], [b_par])

    for gi, gsrc in enumerate((q_norm_a, k_norm_a, q_norm_c, k_norm_c)):
        for h0 in (0, 64):
            setup_dma(gains[h0:h0 + 64, gi:gi + 1], gsrc.rearrange("o d -> d o"))
    setup_dma(vgain[:], v_norm_b.partition_broadcast(128))
    setup_dma(bsbc[:], b_spatial.partition_broadcast(128))
    setup_dma(esink[:], sink_c.partition_broadcast(128))
    gb = ar.alloc("gb", [128, 4, 64], F32)
    for gi, gsrc in enumerate((q_norm_a, k_norm_a, q_norm_c, k_norm_c)):
        setup_dma(gb[:, gi, :], gsrc.partition_broadcast(128))
    def load_T(src_rows, R, dst):
        k.dma(SP, ds_ld[0], [(stg[0:R, 0:128], src_rows, {})], [], [b_stg])
        pt, bpt = bankA()
        transpose_to(pt[:, 0:R], stg[0:R, 0:128], [b_stg], [bpt], kp=R)
        k.op(DVE, lambda: nc.vector.tensor_copy(out=dst, in_=pt[:, 0:R]), [bpt], [b_const])

    for L in range(2):
        for r in range(2):
            pass
    for L in range(2):
        load_T(b_mod[L].rearrange("(j p) -> j p", p=128), 48, bmT[:, L, :])
        for i in range(3):
            load_T(conv_w[L, i].rearrange("(j p) -> j p", p=128), 44, cwT[:, L, i, :])
        load_T(conv_b[L].rearrange("(j p) -> j p", p=128), 44, cwT[:, L, 3, :])
    cT = ar.alloc("cT", [128, 16], F32)
    load_T(cvec.rearrange("r (c p) -> (r c) p", p=128), 16, cT[:])
    act_op(scT[:].rearrange("p c r -> p r c"), cT[:].rearrange("p (r c) -> p r c", r=2), AF.Silu, [b_const], [b_const])
    for g in range(4):
        k.dma(SP, ds_ld[0], [(stg[:, 0:128], w_spatial[g], {})], [], [b_stg])
        pt, bpt = bankA()
        transpose_to(pt[:, 0:128], stg[:, 0:128], [b_stg], [bpt])
        k.op(DVE, lambda g=g, pt=pt: nc.vector.tensor_copy(out=wsT[:, g, :], in_=pt[:, 0:128]), [bpt], [b_const])

    def mod_loads(L, blks=range(12)):
        res = {}
        for blk in blks:
            def fn(slot, bs, dss, L=L, blk=blk):
                sv = slot[:].rearrange("p (c n) -> p c n", c=KC)
                k.dma(POOL, dss, [(sv, w_mod[L][:, blk * 512:(blk + 1) * 512].rearrange("(c p) n -> p c n", p=128), {})], [], [bs])
            res[blk] = WL(fn)
        return res

    def mod_block(L, blk, wl):
        slot, bs = wget(wl)
        sv = slot[:].rearrange("p (c n) -> p c n", c=KC)
        pt, bpt = bankA()
        for jj in range(4):
            mm_group(pt[:, jj * 2:jj * 2 + 2], [(sv[:, c, jj * 128:(jj + 1) * 128], scT[:, c, :]) for c in range(KC)], [bs, b_const], [bpt])
        for r in range(2):
            k.op(DVE, lambda: nc.vector.tensor_tensor(
                out=modT[:, L, r, blk * 4:blk * 4 + 4], in0=pt[:, 0:8].rearrange("p (j r) -> p r j", r=2)[:, r, :],
                in1=bmT[:, L, blk * 4:blk * 4 + 4], op=ALU.add), [bpt, b_const], [b_modL[L]])

    def mod_finish(L, j0s=(8, 32)):
        for r in range(2):
            for j0 in j0s:
                k.op(DVE, lambda: nc.vector.tensor_scalar(out=modT[:, L, r, j0:j0 + 8], in0=modT[:, L, r, j0:j0 + 8], scalar1=1.0, scalar2=None, op0=ALU.add), [], [b_modL[L]])

    k.op(POOL, lambda: nc.gpsimd.iota(tmp_g[:], pattern=[[1, 32], [0, 64]], base=0, channel_multiplier=0, allow_small_or_imprecise_dtypes=True), [], [b_tg])
    k.op(POOL, lambda: nc.gpsimd.iota(tmp_h[:], pattern=[[0, 32], [1, 64]], base=0, channel_multiplier=0, allow_small_or_imprecise_dtypes=True), [], [b_th])
    k.op(DVE, lambda: nc.vector.tensor_tensor(out=tmp_h[:], in0=tmp_h[:], in1=tmp_g[:], op=ALU.subtract), [b_tg], [b_th])
    k.op(DVE, lambda: nc.vector.scalar_tensor_tensor(out=tmp_g[:], in0=tmp_h[:], scalar=tmp_s[:, 2:3], in1=tmp_g[:], op0=ALU.mult, op1=ALU.add), [b_th, b_ts], [b_tg])
    k.op(DVE, lambda: nc.vector.tensor_scalar(out=tmp_g[:], in0=tmp_g[:], scalar1=tmp_s[:, 4:5], scalar2=None, op0=ALU.mult), [b_ts], [b_tg])
    TWO_PI = float(2 * np.pi)

    def sin_of(dst, shift):
        k.op(DVE, lambda: nc.vector.tensor_scalar(out=tmp_h[:], in0=tmp_g[:], scalar1=shift, scalar2=1.0 / TWO_PI, op0=ALU.add, op1=ALU.mult), [b_tg], [b_th])
        ti = ar_tmp_i2
        k.op(DVE, lambda: nc.vector.tensor_copy(out=ti[:], in_=tmp_h[:]), [b_th], [b_ti2])
        k.op(DVE, lambda: nc.vector.tensor_copy(out=tmp_h[:], in_=ti[:]), [b_ti2], [b_th])
        k.op(DVE, lambda: nc.vector.tensor_scalar(out=tmp_h[:], in0=tmp_h[:], scalar1=-TWO_PI, scalar2=shift, op0=ALU.mult, op1=ALU.add), [], [b_th])
        k.op(DVE, lambda: nc.vector.tensor_tensor(out=tmp_h[:], in0=tmp_h[:], in1=tmp_g[:], op=ALU.add), [b_tg], [b_th])
        for sgn in (1.0, -1.0):
            k.op(DVE, lambda sgn=sgn: nc.vector.tensor_scalar(out=dst, in0=tmp_h[:], scalar1=sgn, scalar2=float(np.pi), op0=ALU.mult, op1=ALU.is_gt), [b_th], [b_rope])
            k.op(DVE, lambda sgn=sgn: nc.vector.scalar_tensor_tensor(out=tmp_h[:], in0=dst, scalar=-sgn * TWO_PI, in1=tmp_h[:], op0=ALU.mult, op1=ALU.add), [b_rope], [b_th])
        k.op(DVE, lambda: nc.vector.tensor_scalar(out=tmp_h[:], in0=tmp_h[:], scalar1=float(np.pi), scalar2=-float(np.pi), op0=ALU.min, op1=ALU.max), [], [b_th])
        act_op(dst, tmp_h[:], AF.Sin, [b_th], [b_rope])

    ar_tmp_i2 = ar.alloc("tmp_i2", [128, 2048], I32)
    b_ti2 = Buf()
    x_stage = [(ar.alloc("xst", [128, 1024], F32), Buf()) for _ in range(2)]
    gx = load_x_gen(xs, TS)
    ml0 = mod_loads(0, range(6))
    sin_of(ropeS[:], 0.0)
    for blk in range(3):
        mod_block(0, blk, ml0[blk])
        next(gx, None)
        next(gx, None)
    sin_of(ropeC[:], float(np.pi / 2))
    for blk in range(3, 6):
        mod_block(0, blk, ml0[blk])
        next(gx, None)
        next(gx, None)
    for _ in gx:
        pass
    mod_finish(0, (8,))
    gmax = ar.alloc("gmax", [128, 4], F32)
    k.op(DVE, lambda: nc.vector.tensor_reduce(out=gmax[:], in_=gb[:], axis=mybir.AxisListType.X, op=ALU.max, apply_absolute_value=True), [b_par], [b_const])
    for L in range(2):
        k.op(DVE, lambda L=L: nc.vector.scalar_tensor_tensor(out=negB[:, L:L + 1], in0=gmax[:, 2 * L:2 * L + 1], scalar=-8.0, in1=gmax[:, 2 * L + 1:2 * L + 2], op0=ALU.mult, op1=ALU.mult), [b_const], [b_const])
    act_op(esink[:], esink[:], AF.Exp, [b_const], [b_const, b_par], bias=negB[:, 1:2])


    dump("modT", modT[:], [128, 2, 2, 48], F32)
    dump("ropeC", ropeC[:], [128, TS], F32)
    dump("ropeS", ropeS[:], [128, TS], F32)
    dump("rmat", rmat[:], [128, 128], F32)
    dump("cwT", cwT[:], [128, 2, 4, 44], F32)
    dump("negB", negB[:], [128, 2], F32)
    dump("esink", esink[:], [128, 16], F32)
    k.barrier()
    ar.top = UNION

    def new_sets(n):
        sets = [((ar.alloc("tB", [128, 512], BF16), Buf()), (ar.alloc("tF", [128, 512], F32), Buf()), (ar.alloc("tS", [128, 8], F32), Buf()))
                for _ in range(n)]
        st = [0]

        def nxt():
            i = st[0] % n
            st[0] += 1
            return sets[i]
        return nxt

    def adaln(L, r, sel, t0, n, dst, dst_col, dst_bufs, nxt):
        jsh, jsc = (0, 8) if sel == 0 else (24, 32)
        pss, bpss = bankA()
        bm = b_modL[L]
        for c in range(KC):
            (sq, bsq), _, _ = nxt()
            if c % 3 == 0:
                act_op(sq[:, 0:n], xT[:, c, t0:t0 + n], AF.Square, xb([c], t0, t0 + n), [bsq])
            elif c % 3 == 1:
                k.op(DVE, lambda: nc.vector.tensor_tensor(out=sq[:, 0:n], in0=xT[:, c, t0:t0 + n], in1=xT[:, c, t0:t0 + n], op=ALU.mult), xb([c], t0, t0 + n), [bsq])
            else:
                k.op(POOL, lambda: nc.gpsimd.tensor_tensor(out=sq[:, 0:n], in0=xT[:, c, t0:t0 + n], in1=xT[:, c, t0:t0 + n], op=ALU.mult), xb([c], t0, t0 + n) + [b_phase], [bsq])
            k.op(PE, lambda: nc.tensor.matmul(pss[:, 0:n], lhsT=ones[:], rhs=sq[:, 0:n], start=(c == 0), stop=(c == KC - 1)), [bsq, b_const], [bpss])
        act_op(pss[:, 0:n], pss[:, 0:n], AF.Ln, [], [bpss], bias=EPS, scale=1.0 / D)
        act_op(pss[:, 0:n], pss[:, 0:n], AF.Exp, [], [bpss], scale=-0.5)
        for c in range(KC):
            _, (tm, btm), _ = nxt()
            k.op(DVE, lambda: nc.vector.scalar_tensor_tensor(
                out=tm[:, 0:n], in0=xT[:, c, t0:t0 + n], scalar=modT[:, L, r, jsc + c:jsc + c + 1], in1=pss[:, 0:n],
                op0=ALU.mult, op1=ALU.mult), xb([c], t0, t0 + n) + [bpss, bm], [btm])
            act_op(dst[:, c, dst_col:dst_col + n], tm[:, 0:n], AF.Identity, [btm, bm], [dst_bufs[c]],
                   bias=modT[:, L, r, jsh + c:jsh + c + 1])

    def adaln_gen(L, r, sel, t0, n, dst, dst_col, dst_bufs, nxt):
        jsh, jsc = (0, 8) if sel == 0 else (24, 32)
        pss, bpss = bankB()
        bm = b_modL[L]
        for c in range(KC):
            (sq, bsq), _, _ = nxt()
            if c % 3 == 0:
                act_op(sq[:, 0:n], xT[:, c, t0:t0 + n], AF.Square, xb([c], t0, t0 + n), [bsq])
            elif c % 3 == 1:
                k.op(DVE, lambda: nc.vector.tensor_tensor(out=sq[:, 0:n], in0=xT[:, c, t0:t0 + n], in1=xT[:, c, t0:t0 + n], op=ALU.mult), xb([c], t0, t0 + n), [bsq])
            else:
                k.op(POOL, lambda: nc.gpsimd.tensor_tensor(out=sq[:, 0:n], in0=xT[:, c, t0:t0 + n], in1=xT[:, c, t0:t0 + n], op=ALU.mult), xb([c], t0, t0 + n) + [b_phase], [bsq])
            yield
            k.op(PE, lambda: nc.tensor.matmul(pss[:, 0:n], lhsT=ones[:], rhs=sq[:, 0:n], start=(c == 0), stop=(c == KC - 1)), [bsq, b_const], [bpss])
        act_op(pss[:, 0:n], pss[:, 0:n], AF.Ln, [], [bpss], bias=EPS, scale=1.0 / D)
        act_op(pss[:, 0:n], pss[:, 0:n], AF.Exp, [], [bpss], scale=-0.5)
        yield
        for c in range(KC):
            _, (tm, btm), _ = nxt()
            k.op(DVE, lambda: nc.vector.scalar_tensor_tensor(
                out=tm[:, 0:n], in0=xT[:, c, t0:t0 + n], scalar=modT[:, L, r, jsc + c:jsc + c + 1], in1=pss[:, 0:n],
                op0=ALU.mult, op1=ALU.mult), xb([c], t0, t0 + n) + [bpss, bm], [btm])
            act_op(dst[:, c, dst_col:dst_col + n], tm[:, 0:n], AF.Identity, [btm, bm], [dst_bufs[c]],
                   bias=modT[:, L, r, jsh + c:jsh + c + 1])
            if c % 2 == 1:
                yield

    def run_group(grp):
        S = grp == "S"
        T = TS if S else TP
        r = 0 if S else 1
        nblk = T // 512
        seqs = [(0, 2048)] if S else [(i * 256, 256) for i in range(4)]

        if stop == "setup":
            raise _Stop()
        mark = ar.top
        nonlocal x_stage
        if not S:
            x_stage = [(ar.alloc("xst", [128, 1024], F32), Buf()) for _ in range(2)]
            load_x(xp, T)
            k.barrier()
        ar.top = mark

        for L in range(2):
            even = L == 0
            bm = b_modL[L]
            NQC = 4 if even else 8
            NKC = 1 if even else 2
            NKV = 2 * NKC
            TK = T + 512 if S else T
            w_in = w_in_even if even else w_in_odd
            w_out = w_out_even if even else w_out_odd
            gq = gains[:, 0:1] if even else gains[:, 2:3]
            gk = gains[:, 1:2] if even else gains[:, 3:4]
            QOFF, KOFF = 0, (512 if even else 1024)
            WKV = 256 * NKC
            WV = 128 * NKC

            def q_load(m):
                def fn(slot, bs, dss):
                    sv = slot[:].rearrange("p (c n) -> p c n", c=KC)
                    k.dma(POOL, dss, [(sv[:, :, j * 128 + hf * 64:j * 128 + hf * 64 + 64],
                                       w_in[:, QOFF + m * 512 + hf * 256 + j * 64:QOFF + m * 512 + hf * 256 + j * 64 + 64].rearrange("(c p) d -> p c d", p=128), {})
                                      for hf in range(2) for j in range(4)], [], [bs])
                return WL(fn)

            def col_load(c0, w):
                def fn(slot, bs, dss):
                    sv = slot[:, 0:KC * w].rearrange("p (c n) -> p c n", c=KC)
                    k.dma(POOL, dss, [(sv, w_in[:, c0:c0 + w].rearrange("(c p) n -> p c n", p=128), {})], [], [bs])
                return WL(fn)

            def wout_load(cb):
                def fn(slot, bs, dss):
                    sv = slot[:].rearrange("p (c n) -> p c n", c=KC)
                    ncols = slice(cb * 512, (cb + 1) * 512)
                    pieces = []
                    for m in range(NQC // 4):
                        base = 512 * m
                        pieces.append((sv[0:64, 4 * m:4 * m + 4, :], w_out[base:base + 256, ncols].rearrange("(j d) n -> d j n", d=64), {}))
                        pieces.append((sv[64:128, 4 * m:4 * m + 4, :], w_out[base + 256:base + 512, ncols].rearrange("(j d) n -> d j n", d=64), {}))
                    if even:
                        pieces.append((sv[:, 4:8, :], w_out[512:1024, ncols].rearrange("(g p) n -> p g n", p=128), {}))
                    k.dma(POOL, dss, pieces, [], [bs])
                return WL(fn)

            def up_load(u):
                def fn(slot, bs, dss):
                    sv = slot[:].rearrange("p (c n) -> p c n", c=KC)
                    j0 = 2 * u
                    k.dma(POOL, dss, [
                        (sv[:, :, 0:256], w_up[L][:, j0 * 128:j0 * 128 + 256].rearrange("(c p) n -> p c n", p=128), {}),
                        (sv[:, :, 256:512], w_up[L][:, FF + j0 * 128:FF + j0 * 128 + 256].rearrange("(c p) n -> p c n", p=128), {}),
                    ], [], [bs])
                return WL(fn)

            def down_load(co):
                def fn(slot, bs, dss):
                    sv = slot[:, 0:FC * 128].rearrange("p (f n) -> p f n", f=FC)
                    k.dma(POOL, dss, [(sv, w_down[L][:, co * 128:(co + 1) * 128].rearrange("(f p) n -> p f n", p=128), {})], [], [bs])
                return WL(fn)

            wl_proj = []
            for blk in range(nblk):
                d_ = {"q": [q_load(m) for m in range(NQC // 4)], "kv": col_load(KOFF, WKV)}
                if even:
                    d_["u"] = col_load(768, 512)
                    d_["vb"] = col_load(1280, 512)
                wl_proj.append(d_)
            wl_mod0b = mod_loads(0, range(6, 12)) if (S and even) else {}
            wl_mod1 = mod_loads(1) if (S and even) else {}
            wl_out = [wout_load(cb) for cb in range(2)]
            nsb = 2 if S else 1
            wl_ffn = [{"u": [up_load(u) for u in range(FC // 2)], "d": [down_load(co) for co in range(KC)]} for _ in range(nsb)]

            mark = ar.top
            QA = ar.alloc("QA", [128, 8, T], BF16)
            mark_qa = ar.top
            KT = ar.alloc("KT", [128, NKC, TK], BF16)
            VA = ar.alloc("VA", [128, TK // 128, NKV, 128], BF16)
            b_QA = [[Buf() for _ in range(T // 128)] for _ in range(8)]
            b_KT = [[Buf() for _ in range(TK // 128)] for _ in range(NKC)]
            b_VA = [Buf() for _ in range(TK // 128)]
            mark2 = ar.top
            dbl = (not S) or even
            hTs = [ar.alloc("hT", [128, KC, 512], BF16) for _ in range(2 if dbl else 1)]
            b_hTs = [[Buf() for _ in range(KC)] for _ in hTs]
            hT, b_hT = hTs[0], b_hTs[0]
            if dbl:
                asets = [((ar.alloc("aB", [128, 512], BF16), Buf()), (ar.alloc("aF", [128, 512], F32), Buf()), (None, None)) for _ in range(2)]
                as_i = [0]

                def anxt():
                    i = as_i[0] % 2
                    as_i[0] += 1
                    return asets[i]
            kvst = [(ar.alloc("kvst", [128, 256], F32), Buf()) for _ in range(2)]
            nxt = new_sets(3)

            def qa_b(chunks, t0, t1):
                return [b_QA[c][g] for c in chunks for g in range(t0 // 128, (t1 + 127) // 128)]

            k.op(DVE, lambda: nc.vector.memset(VA[:, :, :, 64:128], 1.0), [], b_VA)

            if S:
                ck = cka if even else cks
                cv = cva if even else cvs
                for t in range(4):
                    st_t, b_st = kvst[t % 2]
                    k.dma(SP, ds_cache, [(st_t[:, 0:WV], ck[t * 128:(t + 1) * 128, :], {})], [], [b_st])
                    for m in range(NKC):
                        pt, bpt = bankA()
                        transpose_to(pt[:, 0:128], st_t[:, m * 128:(m + 1) * 128], [b_st], [bpt])
                        k.op(DVE, lambda: nc.vector.tensor_copy(out=KT[:, m, T + t * 128:T + (t + 1) * 128], in_=pt[:, 0:128]), [bpt], [b_KT[m][T // 128 + t]])
                for t in range(4):
                    st_t, b_st = kvst[t % 2]
                    k.dma(SP, ds_cache, [(st_t[:, 0:WV], cv[t * 128:(t + 1) * 128, :], {})], [], [b_st])
                    k.op(DVE, lambda: nc.vector.tensor_copy(
                        out=VA[:, T // 128 + t, :, 0:64], in_=st_t[:, 0:WV].rearrange("p (h d) -> p h d", d=64)), [b_st], [b_VA[T // 128 + t]])

            def g_qk(wl, col, gcol, dst, dbufs, t0, rope, kout):
                slot, bs = wget(wl)
                sv = slot[:].rearrange("p (c n) -> p c n", c=KC) if wl.kind == "q" else slot[:, 0:KC * WKV].rearrange("p (c n) -> p c n", c=KC)
                (sq, bsq), (qn, bqn), _ = nxt()
                ps, bps = bankA()
                mm_group(ps[:], [(sv[:, kc, col:col + 128], hT[:, kc, :]) for kc in range(KC)], [bs] + b_hT, [bps])
                act_op(sq[:], ps[:], AF.Square, [bps], [bsq])
                if rope:
                    act_op(qn[:], ps[:], AF.Identity, [bps, b_const], [bqn], scale=gcol)
                else:
                    k.op(DVE, lambda: nc.vector.tensor_scalar(out=qn[:], in0=ps[:], scalar1=gcol, scalar2=None, op0=ALU.mult), [bps, bsq, b_const], [bqn])
                yield
                pss, bpss = bankA()
                mm_group(pss[:], [(bones[:], sq[:])], [bsq, b_const], [bpss])
                if rope:
                    prot, bprot = bankA()
                    mm_group(prot[:], [(rmat[:], qn[:])], [bqn, b_const], [bprot])
                act_op(pss[:], pss[:], AF.Ln, [], [bpss], bias=EPS, scale=1.0 / 64)
                act_op(pss[:], pss[:], AF.Exp, [], [bpss], scale=-0.5)
                if rope:
                    k.op(DVE, lambda: nc.vector.tensor_tensor(out=prot[:], in0=prot[:], in1=ropeS[:, t0:t0 + 512], op=ALU.mult), [b_const], [bprot])
                    k.op(DVE, lambda: nc.vector.tensor_tensor(out=qn[:], in0=qn[:], in1=ropeC[:, t0:t0 + 512], op=ALU.mult), [b_const], [bqn])
                    k.op(DVE, lambda: nc.vector.tensor_tensor(out=qn[:], in0=qn[:], in1=prot[:], op=ALU.add), [bprot], [bqn])
                    k.op(DVE, lambda: nc.vector.tensor_tensor(out=dst, in0=qn[:], in1=pss[:], op=ALU.mult), [bqn, bpss], dbufs)
                    return
                if kout is None:
                    k.op(DVE, lambda: nc.vector.tensor_tensor(out=dst, in0=qn[:], in1=pss[:], op=ALU.mult), [bqn, bpss], dbufs)
                    return
                k.op(DVE, lambda: nc.vector.tensor_tensor(out=qn[:], in0=qn[:], in1=pss[:], op=ALU.mult), [bpss], [bqn])
                k.op(DVE, lambda: nc.vector.tensor_copy(out=dst, in_=qn[:]), [bqn], dbufs)
                yield
                kout(qn, bqn)

            def g_v(wl, s, t0):
                slot, bs = wget(wl)
                sv = slot[:, 0:KC * WKV].rearrange("p (c n) -> p c n", c=KC)
                ps, bps = bankA()
                mm_group(ps[:, 0:WV], [(hT[:, kc, s * 128:(s + 1) * 128], sv[:, kc, WV:2 * WV]) for kc in range(KC)], [bs] + b_hT, [bps])
                ch = t0 // 128 + s
                if S:
                    k.op(DVE, lambda: nc.vector.tensor_copy(out=VA[:, ch, :, 0:64], in_=ps[:, 0:WV].rearrange("p (h d) -> p h d", d=64)), [bps], [b_VA[ch]])
                else:
                    st_t, b_st = kvst[s % 2]
                    act_op(st_t[:, 0:WV], ps[:, 0:WV], AF.Copy, [bps], [b_st])
                    k.op(DVE, lambda: nc.vector.tensor_copy(out=VA[:, ch, :, 0:64], in_=st_t[:, 0:WV].rearrange("p (h d) -> p h d", d=64)), [b_st], [b_VA[ch]])
                    dst = (nva if even else nvs)[t0 + s * 128:t0 + (s + 1) * 128, :]
                    k.dma(SP, ds_kv[s % 2], [(dst, st_t[:, 0:WV], {})], [b_st], [])
                return
                yield

            def g_u(wl, g, t0):
                slot, bs = wget(wl)
                sv = slot[:].rearrange("p (c n) -> p c n", c=KC)
                ps, bps = bankA()
                mm_group(ps[:], [(sv[:, kc, g * 128:(g + 1) * 128], hT[:, kc, :]) for kc in range(KC)], [bs] + b_hT, [bps])
                act_op(QA[:, 4 + g, t0:t0 + 512], ps[:], AF.Gelu_apprx_tanh, [bps], qa_b([4 + g], t0, t0 + 512))
                return
                yield

            def g_vb(wl, s, t0):
                slot, bs = wget(wl)
                sv = slot[:].rearrange("p (c n) -> p c n", c=KC)
                (vn, bvn), (gv, bgv), (sst, bsst) = nxt()
                ps, bps = bankA()
                mm_group(ps[:], [(hT[:, kc, s * 128:(s + 1) * 128], sv[:, kc, :]) for kc in range(KC)], [bs] + b_hT, [bps])
                act_op(gv[:], ps[:], AF.Gelu_apprx_tanh, [bps], [bgv])
                act_op(vn[:], gv[:], AF.Square, [bgv], [bvn, bsst], accum_out=sst[:, 0:1])
                act_op(sst[:, 1:2], sst[:, 0:1], AF.Ln, [], [bsst], bias=EPS, scale=1.0 / 512)
                act_op(sst[:, 2:3], sst[:, 1:2], AF.Exp, [], [bsst], scale=-0.5)
                k.op(DVE, lambda: nc.vector.scalar_tensor_tensor(out=vn[:], in0=gv[:], scalar=sst[:, 2:3], in1=vgain[:], op0=ALU.mult, op1=ALU.mult), [bgv, bsst, b_const], [bvn])
                yield
                pm, bpm = bankA()
                for g in range(4):
                    mm_group(pm[:, g * 128:(g + 1) * 128], [(vn[:, g * 128:(g + 1) * 128], wsT[:, g, :])], [bvn, b_const], [bpm])
                k.op(DVE, lambda: nc.vector.tensor_tensor(out=pm[:], in0=pm[:], in1=bsbc[:], op=ALU.add), [b_const], [bpm])
                tt0 = t0 + s * 128
                qv = QA[:, 4:8, tt0:tt0 + 128]
                k.op(DVE, lambda: nc.vector.tensor_tensor(out=qv, in0=pm[:].rearrange("p (g n) -> p g n", g=4), in1=qv, op=ALU.mult), [bpm], qa_b(range(4, 8), tt0, tt0 + 128))

            def mk_kout(m, t0):
                def kout(qn, bqn):
                    for s in range(4):
                        st_t, b_st = kvst[s % 2]
                        pt, bpt = bankA()
                        transpose_to(pt[:, 0:128], qn[:, s * 128:(s + 1) * 128], [bqn], [bpt])
                        k.op(DVE, lambda: nc.vector.tensor_copy(out=st_t[:, 0:128], in_=pt[:, 0:128]), [bpt], [b_st])
                        dst = (nka if even else nks)[t0 + s * 128:t0 + (s + 1) * 128, m * 128:(m + 1) * 128]
                        k.dma(SP, ds_kv[s % 2], [(dst, st_t[:, 0:128], {})], [b_st], [])
                return kout

            for blk in range(nblk):
                t0 = blk * 512
                hT, b_hT = hTs[blk % len(hTs)], b_hTs[blk % len(hTs)]
                if blk == 0 or not dbl:
                    adaln(L, r, 0, t0, 512, hT, 0, b_hT, nxt)
                if blk == 0:
                    dump(f"hT_{grp}{L}", hT[:], [128, KC, 512], BF16)
                wls = wl_proj[blk]
                gens = []
                if dbl and blk + 1 < nblk:
                    gens.append(adaln_gen(L, r, 0, t0 + 512, 512, hTs[(blk + 1) % 2], 0, b_hTs[(blk + 1) % 2], anxt))
                for m in range(NQC // 4):
                    wls["q"][m].kind = "q"
                    for j in range(4):
                        c = 4 * m + j
                        gens.append(g_qk(wls["q"][m], j * 128, gq, QA[:, c, t0:t0 + 512], qa_b([c], t0, t0 + 512), t0, S, None))
                wls["kv"].kind = "kv"
                for m in range(NKC):
                    gens.append(g_qk(wls["kv"], m * 128, gk, KT[:, m, t0:t0 + 512], [b_KT[m][t0 // 128 + g] for g in range(4)], t0, S,
                                     None if S else mk_kout(m, t0)))
                for s in range(4):
                    gens.append(g_v(wls["kv"], s, t0))
                if even:
                    for g in range(4):
                        gens.append(g_u(wls["u"], g, t0))
                    for s_ in range(4):
                        gens.append(g_vb(wls["vb"], s_, t0))
                run_pipe(gens)

            if stop == f"proj_{grp}{L}":
                k.barrier()
                raise _Stop()
            dump(f"QA_{grp}{L}", QA[:], [128, 8, T], BF16)
            dump(f"KT_{grp}{L}", KT[:], [128, NKC, TK], BF16)
            dump(f"VA_{grp}{L}", VA[:], [128, TK // 128, NKV, 128], BF16)
            k.barrier()
            ar.top = mark2

            KTb = ar.alloc("KTb", [128, NKC, TK], BF16)
            b_KTb = Buf()
            allkt = [b for row in b_KT for b in row]
            k.op(DVE, lambda: nc.vector.memset(KTb[0:64, :, :], 0.0), [], [b_KTb])
            k.op(DVE, lambda: nc.vector.tensor_copy(out=KTb[64:128, :, :], in_=KT[64:128, :, :]), allkt, [b_KTb])
            k.op(DVE, lambda: nc.vector.memset(KT[64:128, :, :], 0.0), [], allkt)
            NPT = 3
            PT = [(ar.alloc("PT", [128, 2, 512], BF16), Buf()) for _ in range(NPT)]
            rl = [(ar.alloc("rl", [128, 512], F32), Buf()) for _ in range(1)]
            pt_i = [0]
            rl_i = [0]
            nb = negB[:, L:L + 1]
            units = []
            for c in range(NQC):
                for hf in range(2):
                    if S:
                        for rg in range(4):
                            chunks = [(T // 128 + t, 0, 512, False, False) for t in range(4)]
                            if even:
                                chunks += [(kb, 0, 512, False, False) for kb in range(16)]
                            else:
                                for kb in range(max(0, 4 * rg - 1), min(16, 4 * rg + 5)):
                                    lo = max(kb - 1, 4 * rg)
                                    hi = min(kb + 1, 4 * rg + 3)
                                    chunks.append((kb, (lo - 4 * rg) * 128, (hi - lo + 1) * 128, lo == kb - 1, hi == kb + 1))
                            units.append({"c": c, "hf": hf, "q0": rg * 512, "nq": 512, "chunks": chunks})
                    else:
                        for (s0, sl) in seqs:
                            units.append({"c": c, "hf": hf, "q0": s0, "nq": sl, "chunks": [(s0 // 128 + i, 0, sl, False, False) for i in range(sl // 128)]})
            ntiles = sum(len(u["chunks"]) for u in units)
            extras = [(lambda blk=blk: mod_block(0, blk, wl_mod0b[blk])) for blk in sorted(wl_mod0b)]
            extras += [(lambda blk=blk: mod_block(1, blk, wl_mod1[blk])) for blk in sorted(wl_mod1)]
            ex_every = max(1, ntiles // (len(extras) + 1)) if extras else 0

            def emit_qk_group(u, idxs):
                c, hf, q0 = u["c"], u["hf"], u["q0"]
                m = c // 4
                ktsrc = KT if hf == 0 else KTb
                _, c0, ncol, mlo, mhi = u["chunks"][idxs[0]]
                pt, bpt = PT[pt_i[0] % NPT]
                pt_i[0] += 1
                if len(idxs) == 2:
                    b0 = bankPair()
                    banks = [b0, b0 + 1]
                else:
                    _, bb = bankA()
                    banks = [b_ps.index(bb)]
                for j, i in enumerate(idxs):
                    kc_i = u["chunks"][i][0]
                    pso = psum[banks[j]]

                    def qk_fn():
                        inst = nc.tensor.matmul(pso[:, c0:c0 + ncol], lhsT=ktsrc[:, m, kc_i * 128:(kc_i + 1) * 128], rhs=QA[:, c, q0 + c0:q0 + c0 + ncol],
                                                start=True, stop=not (mlo or mhi))
                        if mlo:
                            inst = nc.tensor.matmul(pso[:, c0:c0 + 128], lhsT=ident_bf[:], rhs=maskB[:], start=False, stop=not mhi)
                        if mhi:
                            e0 = c0 + ncol - 128
                            inst = nc.tensor.matmul(pso[:, e0:e0 + 128], lhsT=ident_bf[:], rhs=maskA[:], start=False, stop=True)
                        return inst
                    k.op(PE, qk_fn, qa_b([c], q0, q0 + u["nq"]) + [b_KT[m][kc_i], b_KTb, b_const], [b_ps[banks[j]]])
                if len(idxs) == 2:
                    act_op(pt[:, :, c0:c0 + ncol], psall[:, banks[0]:banks[0] + 2, c0:c0 + ncol], AF.Exp, [b_ps[banks[0]], b_ps[banks[1]], b_const], [bpt], bias=nb, scale=0.125)
                else:
                    act_op(pt[:, 0, c0:c0 + ncol], psum[banks[0]][:, c0:c0 + ncol], AF.Exp, [b_ps[banks[0]], b_const], [bpt], bias=nb, scale=0.125)
                return [(u, i, pt[:, j, :], bpt) for j, i in enumerate(idxs)]

            def emit_pv(item):
                u, i, pt, bpt = item
                c, hf, q0, nq = u["c"], u["hf"], u["q0"], u["nq"]
                m = c // 4
                kv = 2 * m + hf
                head = 8 * m + 4 * hf + (c % 4)
                p0 = 64 * hf
                kc_i, c0, ncol, _, _ = u["chunks"][i]
                if i == 0:
                    u["acc"] = bankB()
                acc, bacc = u["acc"]
                last = i == len(u["chunks"]) - 1
                k.op(PE, lambda: nc.tensor.matmul(acc[:, c0:c0 + ncol], lhsT=VA[:, kc_i, kv, :], rhs=pt[:, c0:c0 + ncol], start=(i == 0), stop=last),
                     [bpt, b_VA[kc_i]], [bacc])
                if not last:
                    return
                norm_q.append((tcount_ref[0] + NORM_DELAY, lambda: emit_norm(u)))

            def emit_norm(u):
                c, hf, q0, nq = u["c"], u["hf"], u["q0"], u["nq"]
                m = c // 4
                head = 8 * m + 4 * hf + (c % 4)
                p0 = 64 * hf
                acc, bacc = u["acc"]
                rt, brt = rl[0]
                rl_i[0] += 1
                if not even and S and (u["q0"] // 512 + u["hf"]) % 2 == 0:
                    k.op(DVE, lambda: nc.vector.tensor_scalar(out=rt[64:128, 0:nq], in0=acc[64:128, 0:nq], scalar1=esink[64:128, head:head + 1], scalar2=None, op0=ALU.add), [bacc, b_const], [brt])
                    k.op(DVE, lambda: nc.vector.reciprocal(out=rt[64:128, 0:nq], in_=rt[64:128, 0:nq]), [], [brt])
                elif not even:
                    act_op(rt[64:128, 0:nq], acc[64:128, 0:nq], AF.Ln, [bacc, b_const], [brt], bias=esink[64:128, head:head + 1])
                    act_op(rt[64:128, 0:nq], rt[64:128, 0:nq], AF.Exp, [], [brt], scale=-1.0)
                elif not S:
                    act_op(rt[64:128, 0:nq], acc[64:128, 0:nq], AF.Ln, [bacc], [brt])
                    act_op(rt[64:128, 0:nq], rt[64:128, 0:nq], AF.Exp, [], [brt], scale=-1.0)
                else:
                    k.op(DVE, lambda: nc.vector.reciprocal(out=rt[64:128, 0:nq], in_=acc[64:128, 0:nq]), [bacc], [brt])
                k.op(DVE, lambda: nc.vector.tensor_tensor(out=QA[p0:p0 + 64, c, q0:q0 + nq], in0=acc[0:64, 0:nq], in1=rt[64:128, 0:nq], op=ALU.mult), [bacc, brt], qa_b([c], q0, q0 + nq))

            LAG = 3
            pend = []
            tcount = 0
            tcount_ref = [0]
            norm_q = []
            NORM_DELAY = 4 if S else 2
            for u in units:
                ch = u["chunks"]
                i = 0
                while i < len(ch):
                    if i + 1 < len(ch) and ch[i][1:3] == ch[i + 1][1:3] and not (ch[i][3] or ch[i][4] or ch[i + 1][3] or ch[i + 1][4]):
                        idxs = [i, i + 1]
                    else:
                        idxs = [i]
                    pend.append(emit_qk_group(u, idxs))
                    i += len(idxs)
                    while len(pend) > 2:
                        for it in pend.pop(0):
                            emit_pv(it)
                    for _ in idxs:
                        tcount += 1
                        tcount_ref[0] = tcount
                        if extras and tcount % ex_every == 0:
                            extras.pop(0)()
                    while norm_q and norm_q[0][0] <= tcount:
                        norm_q.pop(0)[1]()
            while pend:
                for it in pend.pop(0):
                    emit_pv(it)
            while norm_q:
                norm_q.pop(0)[1]()
            while extras:
                extras.pop(0)()
            if wl_mod1:
                mod_finish(0, (32,))
                mod_finish(1)

            if stop == f"att_{grp}{L}":
                k.barrier()
                raise _Stop()
            dump(f"AT_{grp}{L}", QA[:], [128, 8, T], BF16)
            k.barrier()
            ar.top = mark2

            if S:
                sblocks = [[(0, 1024, "zero", "halo")], [(1024, 2048, "saved", "zero")]]
            else:
                sblocks = [[(i * 256, (i + 1) * 256, "zero", "zero") for i in range(4)]]
            ar.top = mark_qa
            hi_mark = ar.hi
            ncols_max = max(sum(b - a + 2 for (a, b, _, _) in segs) for segs in sblocks)
            h2 = ar.alloc_hi("h2", [128, KC, ncols_max], BF16)
            b_h2 = [Buf() for _ in range(KC)]
            fsets = []
            for _ in range(2):
                fsets.append(((ar.alloc_hi("fB", [128, 512], BF16), Buf()), (ar.alloc_hi("fF", [128, 512], F32), Buf()), (None, None)))
            fs_i = [0]

            def fnxt():
                i = fs_i[0] % 2
                fs_i[0] += 1
                return fsets[i]

            def ffn_plan(segs):
                pieces = []
                windows = []
                off = 0
                for (a, b, lh, rh) in segs:
                    pre = []
                    if lh == "zero":
                        pre.append(lambda off=off: k.op(DVE, lambda: nc.vector.memset(h2[:, :, off:off + 1], 0.0), [], b_h2))
                    elif lh == "saved":
                        pre.append(lambda off=off: k.op(DVE, lambda: nc.vector.tensor_copy(out=h2[:, :, off:off + 1], in_=halo[:, :, 0:1]), [b_halo], b_h2))
                    if rh == "zero":
                        pre.append(lambda off=off, a=a, b=b: k.op(DVE, lambda: nc.vector.memset(h2[:, :, off + 1 + b - a:off + 2 + b - a], 0.0), [], b_h2))
                    tb = b + (1 if rh == "halo" else 0)
                    t = a
                    first = True
                    while t < tb:
                        n = min(512, tb - t)
                        last = t + n >= tb

                        def mk(t=t, n=n, off=off, a=a, b=b, pre=(pre if first else []), post_halo=(last and rh == "halo")):
                            for f in pre:
                                f()
                            yield from adaln_gen(L, r, 1, t, n, h2, off + 1 + (t - a), b_h2, fnxt)
                            if post_halo:
                                k.op(DVE, lambda: nc.vector.tensor_copy(out=halo[:, :, 0:1], in_=h2[:, :, off + b - a:off + b - a + 1]), b_h2, [b_halo])
                        pieces.append((t + n, mk))
                        first = False
                        t += n
                    nw = (b - a + 509) // 510
                    base, rem = divmod(b - a, nw)
                    o = a
                    for wi in range(nw):
                        ln = base + (1 if wi < rem else 0)
                        windows.append((off + (o - a), ln + 2, o, ln))
                        o += ln
                    off += b - a + 2
                return pieces, windows

            plans = [ffn_plan(segs) for segs in sblocks]
            nstep = 2 if S else 3

            def step_gens(active, n):
                for _ in range(n):
                    if not active:
                        return
                    g = active[0]
                    try:
                        next(g)
                    except StopIteration:
                        active.pop(0)

            def drain(active):
                while active:
                    step_gens(active, 1)

            slots = [wget(wl_out[cb]) for cb in range(2)]
            pieces0 = list(plans[0][0])
            active = []
            for blk in range(nblk):
                t0 = blk * 512
                for co in range(KC):
                    slot, bs = slots[co // 4]
                    sv = slot[:].rearrange("p (c n) -> p c n", c=KC)
                    oc = co % 4
                    ps, bps = bankA()
                    mm_group(ps[:], [(sv[:, kc, oc * 128:(oc + 1) * 128], QA[:, kc, t0:t0 + 512]) for kc in range(KC)], [bs] + qa_b(range(8), t0, t0 + 512), [bps])
                    k.op(DVE, lambda: nc.vector.scalar_tensor_tensor(
                        out=xT[:, co, t0:t0 + 512], in0=ps[:], scalar=modT[:, L, r, 16 + co:17 + co], in1=xT[:, co, t0:t0 + 512],
                        op0=ALU.mult, op1=ALU.add), [bps, bm], xb([co], t0, t0 + 512))
                    step_gens(active, nstep)
                while pieces0 and pieces0[0][0] <= t0 + 512:
                    active.append(pieces0.pop(0)[1]())
            while pieces0:
                active.append(pieces0.pop(0)[1]())
            drain(active)

            if stop == f"wout_{grp}{L}":
                k.barrier()
                raise _Stop()
            dump(f"XM_{grp}{L}", xT[:, :, 0:T], [128, KC, T], F32)
            k.barrier()
            ar.top = mark

            actT = ar.alloc("actT", [128, FC, 1024], BF16)
            NZ = 7
            zt = [(ar.alloc("zt", [128, 512], F32), Buf()) for _ in range(NZ)]
            zi = [0]

            def ztmp():
                i = zi[0] % NZ
                zi[0] += 1
                return zt[i]

            b_act = [[Buf() for _ in plans[0][1]] for _ in range(FC)]
            for sbi, segs in enumerate(sblocks):
                windows = plans[sbi][1]
                assert len(windows) == len(plans[0][1])
                sb0 = segs[0][0]
                pend_fin = []
                for u in range(FC // 2):
                    slot, bs = wget(wl_ffn[sbi]["u"][u])
                    sv = slot[:].rearrange("p (c n) -> p c n", c=KC)
                    j0 = 2 * u
                    for jj in range(2):
                        j = j0 + jj
                        for wi, (c0, N, o, ln) in enumerate(windows):
                            res = []
                            parts = []
                            for part in range(2):
                                col = part * 256 + jj * 128
                                jc = j + part * FC
                                ps, bps = bankA()
                                mm_group(ps[:, 0:N], [(sv[:, kc, col:col + 128], h2[:, kc, c0:c0 + N]) for kc in range(KC)], [bs] + b_h2, [bps])
                                tz, btz = ztmp()
                                act_op(tz[:, 0:ln], ps[:, 1:1 + ln], AF.Identity, [bps, b_const], [btz], bias=cwT[:, L, 3, jc:jc + 1], scale=cwT[:, L, 1, jc:jc + 1])
                                parts.append((ps, bps, tz, btz, jc))
                                res.append((tz, btz))
                            for tap, lo in ((0, 0), (2, 2)):
                                for (ps, bps, tz, btz, jc) in parts:
                                    k.op(DVE, lambda: nc.vector.scalar_tensor_tensor(out=tz[:, 0:ln], in0=ps[:, lo:lo + ln], scalar=cwT[:, L, tap, jc:jc + 1], in1=tz[:, 0:ln], op0=ALU.mult, op1=ALU.add), [bps, b_const], [btz])
                            (ta_, bta), (tg_, btg) = res
                            sg, bsg = ztmp()

                            def fin(ta_=ta_, bta=bta, tg_=tg_, btg=btg, sg=sg, bsg=bsg, j=j, o=o, ln=ln, wi=wi):
                                act_op(sg[:, 0:ln], tg_[:, 0:ln], AF.Silu, [btg], [bsg])
                                k.op(POOL, lambda: nc.gpsimd.tensor_tensor(out=actT[:, j, o - sb0:o - sb0 + ln], in0=ta_[:, 0:ln], in1=sg[:, 0:ln], op=ALU.mult), [bta, bsg], [b_act[j][wi]])
                            if pend_fin:
                                pend_fin.pop(0)()
                            pend_fin.append(fin)
                while pend_fin:
                    pend_fin.pop(0)()
                active = [mk() for (_, mk) in plans[sbi + 1][0]] if sbi + 1 < len(sblocks) else []
                for co in range(KC):
                    slot, bs = wget(wl_ffn[sbi]["d"][co])
                    sv = slot[:, 0:FC * 128].rearrange("p (f n) -> p f n", f=FC)
                    for wi, (c0, N, o, ln) in enumerate(windows):
                        ps, bps = bankA()
                        mm_group(ps[:, 0:ln], [(sv[:, f, :], actT[:, f, o - sb0:o - sb0 + ln]) for f in range(FC)], [bs] + [b_act[f][wi] for f in range(FC)], [bps])
                        k.op(DVE, lambda: nc.vector.scalar_tensor_tensor(
                            out=xT[:, co, o:o + ln], in0=ps[:, 0:ln], scalar=modT[:, L, r, 40 + co:41 + co], in1=xT[:, co, o:o + ln],
                            op0=ALU.mult, op1=ALU.add), [bps, bm], xb([co], o, o + ln))
                        step_gens(active, 2)
                drain(active)
            k.barrier()
            ar.top = mark
            ar.hi = hi_mark
            if stop == f"ffn_{grp}{L}":
                raise _Stop()
            dump(f"XF_{grp}{L}", xT[:, :, 0:T], [128, KC, T], F32)

        mark = ar.top
        x_stage = [(ar.alloc("xst", [128, 1024], F32), Buf()) for _ in range(2)]
        store_x(ys if S else yp, T)
        k.barrier()
        ar.top = mark

    try:
        run_group("S")
        run_group("P")
    except _Stop:
        pass
    k.final()
    return nc


_NC_CACHE = {}


def kernel(x_prompt, x_sample, cache_k_attn, cache_v_attn, cache_k_swa, cache_v_swa, c, c_ctx,
           w_mod, b_mod, w_in_even, q_norm_a, k_norm_a, v_norm_b, w_spatial, b_spatial, w_out_even,
           w_in_odd, q_norm_c, k_norm_c, sink_c, w_out_odd, w_up, conv_w, conv_b, w_down):
    f = lambda a: np.ascontiguousarray(np.asarray(a, dtype=np.float32))
    if "nc" not in _NC_CACHE:
        _NC_CACHE["nc"] = build()
    nc = _NC_CACHE["nc"]
    shared = {
        "w_mod": f(w_mod), "b_mod": f(b_mod), "w_in_even": f(w_in_even[0]),
        "q_norm_a": f(q_norm_a), "k_norm_a": f(k_norm_a), "v_norm_b": f(v_norm_b),
        "w_spatial": f(w_spatial[0]), "b_spatial": f(b_spatial).reshape(1, 512),
        "w_out_even": f(w_out_even[0]), "w_in_odd": f(w_in_odd[0]),
        "q_norm_c": f(q_norm_c), "k_norm_c": f(k_norm_c), "sink_c": f(sink_c),
        "w_out_odd": f(w_out_odd[0]), "w_up": f(w_up), "conv_w": f(conv_w), "conv_b": f(conv_b),
        "w_down": f(w_down),
    }
    x_prompt = f(x_prompt); x_sample = f(x_sample); c = f(c); c_ctx = f(c_ctx)
    in_maps = []
    for i in range(N_CORES):
        m = dict(shared)
        m["xs"] = x_sample[i]
        m["xp"] = x_prompt[4 * i:4 * i + 4].reshape(TP, D)
        m["cka"] = f(cache_k_attn[i, 0]).reshape(512, 128)
        m["cva"] = f(cache_v_attn[i, 0]).reshape(512, 128)
        m["cks"] = f(cache_k_swa[i, 0]).reshape(512, 256)
        m["cvs"] = f(cache_v_swa[i, 0]).reshape(512, 256)
        m["cvec"] = np.ascontiguousarray(np.stack([c[i], c_ctx], axis=0))
        in_maps.append(m)
    res = run_bass_kernel_spmd(nc, in_maps, core_ids=list(range(N_CORES)))
    R = res.results
    y_prompt = np.concatenate([R[i]["yp"].reshape(4, 256, D) for i in range(N_CORES)], axis=0)
    y_sample = np.stack([R[i]["ys"] for i in range(N_CORES)], axis=0)
    nka = np.concatenate([R[i]["nka"].reshape(4, 1, 256, 2, 64) for i in range(N_CORES)], axis=0)
    nva = np.concatenate([R[i]["nva"].reshape(4, 1, 256, 2, 64) for i in range(N_CORES)], axis=0)
    nks = np.concatenate([R[i]["nks"].reshape(4, 1, 256, 4, 64) for i in range(N_CORES)], axis=0)
    nvs = np.concatenate([R[i]["nvs"].reshape(4, 1, 256, 4, 64) for i in range(N_CORES)], axis=0)
    return (y_prompt.astype(np.float32), y_sample.astype(np.float32), nka.astype(np.float32),
            nva.astype(np.float32), nks.astype(np.float32), nvs.astype(np.float32))
```
